# Optimizing a Trainium2 kernel written in Bass

```python
import math
import jax, jax.numpy as jnp
from jax import lax
import numpy as np

D_MODEL = 1024
BATCH = 8
SEQ = 4096
DEPTH = 1

NORM_EPS = 1e-6
D_FF = 2816
GDN_HEADS = 4
GDN_HEAD_DIM = 128
GDN_WIDTH = GDN_HEADS * GDN_HEAD_DIM
GDN_CHUNK = 64
CONV_WIDTH = 4
MOBA_HEADS = 8
MOBA_HEAD_DIM = 64
MOBA_WIDTH = MOBA_HEADS * MOBA_HEAD_DIM
MOBA_BLOCK = 256
MOBA_TOPK = 3
MOBA_Q_CHUNK = 128
ROPE_THETA = 500000.0
ROPE_DIM = MOBA_HEAD_DIM // 4
MIX_WIDTH = GDN_WIDTH + MOBA_WIDTH
OFF_GDN_QKV = 3 * GDN_WIDTH
OFF_GDN_Z = 4 * GDN_WIDTH
OFF_GDN_A = OFF_GDN_Z + GDN_HEADS
OFF_GDN_B = OFF_GDN_A + GDN_HEADS
OFF_MOBA_Q = OFF_GDN_B + MOBA_WIDTH
OFF_MOBA_K = OFF_MOBA_Q + MOBA_WIDTH
IN_PROJ_DIM = OFF_MOBA_K + MOBA_WIDTH

kernel_name = "hymba_gdn_moba_macaron"


def rms_norm(x, g, eps=NORM_EPS):
    xf = x.astype(jnp.float32)
    y = xf * lax.rsqrt(jnp.mean(xf * xf, axis=-1, keepdims=True) + eps)
    return (y * g).astype(x.dtype)


def l2_normalize(x, eps=NORM_EPS):
    return x * lax.rsqrt(jnp.sum(x * x, axis=-1, keepdims=True) + eps)


def swiglu_ffn(x, norm_g, w_gate, w_up, w_down):
    h = rms_norm(x, norm_g)
    return (jax.nn.silu(h @ w_gate) * (h @ w_up)) @ w_down


def causal_dwconv(x, w):
    kw = w.shape[0]
    return lax.conv_general_dilated(
        x, w[:, None, :].astype(x.dtype), window_strides=(1,), padding=[(kw - 1, 0)],
        dimension_numbers=('NWC', 'WIO', 'NWC'), feature_group_count=x.shape[-1])


def partial_rope(x, pos):
    half = ROPE_DIM // 2
    inv_freq = jnp.power(jnp.float32(ROPE_THETA), -jnp.arange(half, dtype=jnp.float32) * 2.0 / ROPE_DIM)
    ang = pos.astype(jnp.float32)[:, None] * inv_freq[None, :]
    cos, sin = jnp.cos(ang)[:, None, :], jnp.sin(ang)[:, None, :]
    xr = x[..., :ROPE_DIM].astype(jnp.float32)
    x1, x2 = xr[..., :half], xr[..., half:]
    rot = jnp.concatenate([x1 * cos - x2 * sin, x2 * cos + x1 * sin], axis=-1)
    return jnp.concatenate([rot.astype(x.dtype), x[..., ROPE_DIM:]], axis=-1)


def gated_deltanet(q, k, v, z, a, b, a_log, dt_bias, out_gain):
    bsz, seq, _ = q.shape
    H, Dh, C = GDN_HEADS, GDN_HEAD_DIM, GDN_CHUNK
    n_chunks = seq // C
    f32 = jnp.float32
    q = l2_normalize(q.reshape(bsz, seq, H, Dh).astype(f32)) * (Dh ** -0.5)
    k = l2_normalize(k.reshape(bsz, seq, H, Dh).astype(f32))
    v = v.reshape(bsz, seq, H, Dh).astype(f32)
    beta = jax.nn.sigmoid(b.astype(f32))
    g = -jnp.exp(a_log.astype(f32)) * jax.nn.softplus(a.astype(f32) + dt_bias.astype(f32))

    def chunks(t):
        return t.reshape(bsz, n_chunks, C, H, -1).transpose(0, 3, 1, 2, 4)

    qc, kc, vc = chunks(q), chunks(k), chunks(v)
    beta_c = chunks(beta[..., None])[..., 0]
    gcum = jnp.cumsum(chunks(g[..., None])[..., 0], axis=-1)
    idx = jnp.arange(C)
    causal = idx[:, None] >= idx[None, :]
    strict = idx[:, None] > idx[None, :]
    decay = jnp.exp(jnp.where(causal, gcum[..., :, None] - gcum[..., None, :], -jnp.inf))
    k_beta = kc * beta_c[..., None]
    lower = jnp.where(strict, jnp.einsum('bhnid,bhnjd->bhnij', k_beta, kc) * decay, 0.0)
    rhs = jnp.concatenate([vc * beta_c[..., None], k_beta * jnp.exp(gcum)[..., None]], axis=-1)
    sol = lax.linalg.triangular_solve(jnp.eye(C, dtype=f32) + lower, rhs,
                                      left_side=True, lower=True, unit_diagonal=True)
    u, w = sol[..., :Dh], sol[..., Dh:]
    intra = jnp.einsum('bhnid,bhnjd->bhnij', qc, kc) * decay
    q_dec = qc * jnp.exp(gcum)[..., None]
    k_dec = kc * jnp.exp(gcum[..., -1:] - gcum)[..., None]
    g_last = jnp.exp(gcum[..., -1])
    xs = tuple(jnp.moveaxis(t, 2, 0) for t in (q_dec, k_dec, u, w, intra, g_last))

    def step(state, inp):
        q_i, k_i, u_i, w_i, a_i, gl_i = inp
        v_new = u_i - jnp.einsum('bhck,bhkv->bhcv', w_i, state)
        o_i = jnp.einsum('bhck,bhkv->bhcv', q_i, state) + jnp.einsum('bhij,bhjv->bhiv', a_i, v_new)
        state = state * gl_i[..., None, None] + jnp.einsum('bhck,bhcv->bhkv', k_i, v_new)
        return state, o_i

    _, o = lax.scan(step, jnp.zeros((bsz, H, Dh, Dh), f32), xs)
    o = o.transpose(1, 0, 3, 2, 4).reshape(bsz, seq, H, Dh)
    o = rms_norm(o, out_gain) * jax.nn.silu(z.reshape(bsz, seq, H, Dh).astype(f32))
    return o.reshape(bsz, seq, H * Dh).astype(z.dtype)


def moba_attention(q, k, v, q_gain, k_gain):
    bsz, seq, _ = q.shape
    H, D, BLK, QC = MOBA_HEADS, MOBA_HEAD_DIM, MOBA_BLOCK, MOBA_Q_CHUNK
    n_blocks = -(-seq // BLK)
    seq_p = n_blocks * BLK
    n_qc = seq_p // QC
    topk = min(MOBA_TOPK, n_blocks)
    scale = D ** -0.5
    pos = jnp.arange(seq)
    q = partial_rope(rms_norm(q.reshape(bsz, seq, H, D), q_gain), pos)
    k = partial_rope(rms_norm(k.reshape(bsz, seq, H, D), k_gain), pos)
    v = v.reshape(bsz, seq, H, D)
    pad = [(0, 0), (0, seq_p - seq), (0, 0), (0, 0)]
    q, k, v = (jnp.pad(t, pad).transpose(0, 2, 1, 3) for t in (q, k, v))
    kb = k.reshape(bsz, H, n_blocks, BLK, D)
    vb = v.reshape(bsz, H, n_blocks, BLK, D)
    k_mean = jnp.mean(kb.astype(jnp.float32), axis=3)
    gate = jnp.einsum('bhsd,bhnd->bhsn', q.astype(jnp.float32), k_mean)
    q_block = jnp.arange(seq_p) // BLK
    past = jnp.arange(n_blocks)[None, :] < q_block[:, None]
    gate = jnp.where(past, gate, -jnp.inf)
    _, sel = lax.top_k(gate, topk)

    q_ch = q.reshape(bsz, H, n_qc, QC, D).transpose(0, 2, 1, 3, 4).reshape(bsz * n_qc, H, QC, D)
    sel_ch = sel.reshape(bsz, H, n_qc, QC, topk).transpose(0, 2, 1, 3, 4).reshape(bsz * n_qc, H, QC, topk)
    b_idx = jnp.repeat(jnp.arange(bsz, dtype=jnp.int32), n_qc)
    c_idx = jnp.tile(jnp.arange(n_qc, dtype=jnp.int32), bsz)
    head_ix = jnp.arange(H)[:, None, None]

    def chunk_fn(args):
        qc, selc, bi, ci = args
        kb_b, vb_b = kb[bi], vb[bi]
        k_sel = kb_b[head_ix, selc]
        v_sel = vb_b[head_ix, selc]
        own = (ci * QC) // BLK
        k_own = lax.dynamic_index_in_dim(kb_b, own, axis=1, keepdims=False)
        v_own = lax.dynamic_index_in_dim(vb_b, own, axis=1, keepdims=False)
        valid = jnp.arange(topk) < own
        s_sel = jnp.einsum('hqd,hqtkd->hqtk', qc, k_sel).astype(jnp.float32) * scale
        s_sel = jnp.where(valid[None, None, :, None], s_sel, -jnp.inf)
        q_pos = ci * QC + jnp.arange(QC)
        k_pos = own * BLK + jnp.arange(BLK)
        s_own = jnp.einsum('hqd,hkd->hqk', qc, k_own).astype(jnp.float32) * scale
        s_own = jnp.where(k_pos[None, None, :] <= q_pos[None, :, None], s_own, -jnp.inf)
        p = jax.nn.softmax(jnp.concatenate([s_sel.reshape(H, QC, topk * BLK), s_own], axis=-1), axis=-1)
        p_sel = p[..., :topk * BLK].reshape(H, QC, topk, BLK).astype(v_sel.dtype)
        p_own = p[..., topk * BLK:].astype(v_own.dtype)
        o = jnp.einsum('hqtk,hqtkd->hqd', p_sel, v_sel) + jnp.einsum('hqk,hkd->hqd', p_own, v_own)
        return o.astype(qc.dtype)

    o = lax.map(chunk_fn, (q_ch, sel_ch, b_idx, c_idx))
    o = o.reshape(bsz, n_qc, H, QC, D).transpose(0, 1, 3, 2, 4).reshape(bsz, seq_p, H * D)
    return o[:, :seq]


def setup_inputs(seed: int = 0) -> dict:
    key = jax.random.key(seed)
    ks = jax.random.split(key, 20)
    f32 = jnp.float32

    def normal(k, shape, scale):
        return jax.random.normal(k, shape, f32) * scale

    def gain(k, n):
        return 1.0 + 0.02 * jax.random.normal(k, (DEPTH, n), f32)

    dt = jnp.exp(jax.random.uniform(ks[9], (DEPTH, GDN_HEADS), f32, math.log(1e-3), math.log(1e-1)))
    return {
        "x": normal(ks[0], (BATCH, SEQ, D_MODEL), 1.0),
        "ffn1_norm": gain(ks[1], D_MODEL),
        "ffn1_w_gate": normal(ks[2], (DEPTH, D_MODEL, D_FF), D_MODEL ** -0.5),
        "ffn1_w_up": normal(ks[3], (DEPTH, D_MODEL, D_FF), D_MODEL ** -0.5),
        "ffn1_w_down": normal(ks[4], (DEPTH, D_FF, D_MODEL), D_FF ** -0.5),
        "mix_norm": gain(ks[5], D_MODEL),
        "w_in": normal(ks[6], (DEPTH, D_MODEL, IN_PROJ_DIM), D_MODEL ** -0.5),
        "gdn_conv": normal(ks[7], (DEPTH, CONV_WIDTH, 3 * GDN_WIDTH), CONV_WIDTH ** -0.5),
        "gdn_a_log": jnp.log(jax.random.uniform(ks[8], (DEPTH, GDN_HEADS), f32, 1.0, 16.0)),
        "gdn_dt_bias": dt + jnp.log(-jnp.expm1(-dt)),
        "gdn_out_norm": gain(ks[10], GDN_HEAD_DIM),
        "moba_q_norm": gain(ks[11], MOBA_HEAD_DIM),
        "moba_k_norm": gain(ks[12], MOBA_HEAD_DIM),
        "w_out": normal(ks[13], (DEPTH, MIX_WIDTH, D_MODEL), MIX_WIDTH ** -0.5),
        "ffn2_norm": gain(ks[14], D_MODEL),
        "ffn2_w_gate": normal(ks[15], (DEPTH, D_MODEL, D_FF), D_MODEL ** -0.5),
        "ffn2_w_up": normal(ks[16], (DEPTH, D_MODEL, D_FF), D_MODEL ** -0.5),
        "ffn2_w_down": normal(ks[17], (DEPTH, D_FF, D_MODEL), D_FF ** -0.5),
    }


def reference(x, ffn1_norm, ffn1_w_gate, ffn1_w_up, ffn1_w_down, mix_norm, w_in, gdn_conv,
              gdn_a_log, gdn_dt_bias, gdn_out_norm, moba_q_norm, moba_k_norm, w_out,
              ffn2_norm, ffn2_w_gate, ffn2_w_up, ffn2_w_down):
    for l in range(DEPTH):
        x = x + 0.5 * swiglu_ffn(x, ffn1_norm[l], ffn1_w_gate[l], ffn1_w_up[l], ffn1_w_down[l])
        h = rms_norm(x, mix_norm[l])
        p = h @ w_in[l]
        qkv = jax.nn.silu(causal_dwconv(p[..., :OFF_GDN_QKV], gdn_conv[l]))
        g_q = qkv[..., :GDN_WIDTH]
        g_k = qkv[..., GDN_WIDTH:2 * GDN_WIDTH]
        g_v = qkv[..., 2 * GDN_WIDTH:]
        g_z = p[..., OFF_GDN_QKV:OFF_GDN_Z]
        g_a = p[..., OFF_GDN_Z:OFF_GDN_A]
        g_b = p[..., OFF_GDN_A:OFF_GDN_B]
        o_gdn = gated_deltanet(g_q, g_k, g_v, g_z, g_a, g_b, gdn_a_log[l], gdn_dt_bias[l], gdn_out_norm[l])
        o_moba = moba_attention(p[..., OFF_GDN_B:OFF_MOBA_Q], p[..., OFF_MOBA_Q:OFF_MOBA_K],
                                p[..., OFF_MOBA_K:], moba_q_norm[l], moba_k_norm[l])
        x = x + jnp.concatenate([o_gdn, o_moba], axis=-1) @ w_out[l]
        x = x + 0.5 * swiglu_ffn(x, ffn2_norm[l], ffn2_w_gate[l], ffn2_w_up[l], ffn2_w_down[l])
    return x
```

```python
import contextlib
import math
import numpy as np
import concourse.bass as bass
import concourse.mybir as mybir
from concourse.bass_utils import run_bass_kernel_spmd

F32 = mybir.dt.float32
BF16 = mybir.dt.bfloat16
AF = mybir.ActivationFunctionType
ALU = mybir.AluOpType
AX = mybir.AxisListType

S = 4096
D = 1024
T = 512
NT = S // T
DFF = 2816
EPS = 1e-6
NEG = -30000.0
NPAR = 256

CB_ID, CB_ONES, CB_ONESBD, CB_MS, CB_MIT, CB_CM = 0, 128, 256, 384, 512, 640
NCB = 2688
CF_ID, CF_UI, CF_LST, CF_ONES, CF_RM = 0, 128, 256, 384, 512
NCF = 640
P_N1, P_MIX, P_N2, P_CONV, P_ALOG, P_DTB, P_ONORM, P_QG, P_KG = 0, 8, 16, 24, 72, 88, 104, 232, 233


class Eng:
    def __init__(self, name, eng, in_order=False):
        self.name = name
        self.eng = eng
        self.n = 0
        self.sem = None
        self.waited = {}
        self.in_order = in_order


class Buf:
    __slots__ = ("w", "r", "psum", "name")

    def __init__(self, name="", psum=False):
        self.w = None
        self.r = {}
        self.psum = psum
        self.name = name


def _deps(rd, wr, me):
    need = {}

    def add(e, c):
        if e is me and me.in_order:
            return
        if need.get(e, 0) < c:
            need[e] = c

    for b in rd:
        if b.w is not None:
            add(*b.w)
        if b.psum:
            for e, c in b.r.items():
                if e is not me:
                    add(e, c)
    for b in wr:
        if b.w is not None:
            add(*b.w)
        for e, c in b.r.items():
            add(e, c)
    return need


def op(me, fn, rd=(), wr=()):
    need = _deps(rd, wr, me)
    for e, c in need.items():
        if me.waited.get(e, 0) >= c:
            continue
        me.eng.wait_ge(e.sem, c)
        me.waited[e] = c
    ins = fn()
    me.n += 1
    ins.then_inc(me.sem, 1)
    for b in rd:
        b.r[me] = me.n
    for b in wr:
        b.w = (me, me.n)
        b.r = {}
    return ins


def dma(issuer, dsem, out_ap, in_ap, rd=(), wr=()):
    need = _deps(rd, wr, dsem)
    for e, c in need.items():
        if issuer.waited.get(e, 0) >= c:
            continue
        issuer.eng.wait_ge(e.sem, c)
        issuer.waited[e] = c
    ins = issuer.eng.dma_start(out=out_ap, in_=in_ap)
    dsem.n += 16
    ins.then_inc(dsem.sem, 16)
    for b in rd:
        b.r[dsem] = dsem.n
    for b in wr:
        b.w = (dsem, dsem.n)
        b.r = {}
    return ins


def build(ntiles=NT, debug=False, worder=None):
    nc = bass.Bass("TRN2", target_bir_lowering=False)
    dbg_outs = {}

    def din(name, shape, dt=F32):
        return nc.dram_tensor(name, list(shape), dt, kind="ExternalInput").ap()

    xT = din("xT", [D, S])
    yT = nc.dram_tensor("yT", [D, S], F32, kind="ExternalOutput").ap()
    w1g, w1u, w1d = din("w1g", [D, DFF]), din("w1u", [D, DFF]), din("w1d", [DFF, D])
    w2g, w2u, w2d = din("w2g", [D, DFF]), din("w2u", [D, DFF]), din("w2d", [DFF, D])
    win = din("win", [D, 3592])
    wout = din("wout", [D, D])
    params_d = din("params", [128, NPAR])
    cf_d = din("cf", [128, NCF])
    cb_d = din("cb", [128, NCB])
    cs_d = din("cossin", [128, 2, S])
    gm_d = din("gmk", [128, 16, 2, 256])
    koh_d = din("koh", [64, S])
    von_d = din("vones", [128, 8, 128])
    Kd = nc.dram_tensor("Kd", [8, 128, S], BF16).ap()
    Vd = nc.dram_tensor("Vd", [S, 8, 128], BF16).ap()

    es = contextlib.ExitStack()
    with es:
        def sb(name, shape, dt=F32):
            return es.enter_context(nc.sbuf_tensor("sb_" + name, list(shape), dt))

        PE = Eng("pe", nc.tensor, in_order=True)
        ACT = Eng("act", nc.scalar)
        DVE = Eng("dve", nc.vector)
        POOL = Eng("pool", nc.gpsimd)
        SP = Eng("sp", nc.sync)
        dsems = {}

        def dsem(name):
            if name not in dsems:
                e = Eng(name, None)
                e.sem = es.enter_context(nc.semaphore("d_" + name))
                dsems[name] = e
            return dsems[name]

        for e in (PE, ACT, DVE, POOL, SP):
            e.sem = es.enter_context(nc.semaphore("s_" + e.name))

        NB = 8
        NROT = 6
        banks = []
        for k in range(NB):
            t = es.enter_context(nc.psum_tensor(f"pb{k}", [128, 512], F32))
            banks.append((t, Buf(f"pb{k}", psum=True)))
        def mkpool(idxs):
            st = {"k": 0}

            def alloc():
                k = idxs[st["k"] % len(idxs)]
                st["k"] += 1
                return banks[k]
            return alloc

        P_ALL = mkpool(list(range(8)))
        P_FFN = mkpool([0, 1])
        P_GDN = mkpool([2, 3])
        bank = P_ALL

        class _BV:
            def __init__(self, t):
                self.t = t

            def __getitem__(self, key):
                assert key == (slice(None), slice(0, 128))
                return self.t[:, 0:64].bitcast(BF16)

        def bbank(alloc=None):
            t, B = (alloc or bank)()
            return _BV(t), B

        par = sb("par", [128, NPAR]); parB = Buf("par")
        cf = sb("cf", [128, NCF]); cb = sb("cb", [128, NCB], BF16); cB = Buf("consts")
        kmT = sb("kmT", [128, 8, 32], BF16); kmB = Buf("kmT")
        S_f = sb("S_f", [128, 4, 128]); S_b = sb("S_b", [128, 4, 128], BF16)
        SfB = [Buf(f"Sf{h}") for h in range(4)]; SbB = [Buf(f"Sb{h}") for h in range(4)]
        hist = sb("hist", [128, 12, 3]); histB = [Buf(f"hist{c}") for c in range(12)]
        class Ctx:
            pass
        ctxs = []
        for k in range(2):
            cx = Ctx()
            cx.xs = sb(f"xs{k}", [128, 8, T])
            cx.xsB = [Buf(f"xs{k}_{c}") for c in range(8)]
            ctxs.append(cx)
        hTf = sb("hTf", [128, 8, T], BF16); hBf = [Buf(f"hf{c}") for c in range(8)]
        hTm = sb("hTm", [128, 8, T], BF16); hBm = [Buf(f"hm{c}") for c in range(8)]
        bft2 = sb("bft2", [128, T], BF16); bft2B = Buf("bft2")
        actT = sb("actT", [128, 11, T], BF16); actB = [Buf(f"act{c}") for c in range(11)]
        NSLOT = 4
        wslots = [sb(f"wslot{k}", [128, 4096], BF16) for k in range(NSLOT)]
        wslotB = [Buf(f"wslot{k}") for k in range(NSLOT)]
        NTMP = 6
        tmps = [sb(f"tmp{k}", [128, T]) for k in range(NTMP)]
        tmpB = [Buf(f"tmp{k}") for k in range(NTMP)]

        def mktmp(idxs):
            st = {"k": 0}

            def alloc():
                k = idxs[st["k"] % len(idxs)]
                st["k"] += 1
                return tmps[k], tmpB[k]
            return alloc

        tmpF = mktmp([0, 1])
        tmp = mktmp([2, 3, 4, 5])
        tmpA = mktmp([2, 3])
        tmpB_ = mktmp([4, 5])

        pcx = sb("pcx", [128, T + 3]); pcxB = Buf("pcx")
        qT_b = sb("qT_b", [128, 4, T], BF16); qTB = [Buf(f"qT{h}") for h in range(4)]
        kT_b = sb("kT_b", [128, 4, T], BF16); kTB = [Buf(f"kT{h}") for h in range(4)]
        ktok = sb("ktok", [128, 4, 4, 128], BF16); ktokB = [[Buf() for _ in range(4)] for _ in range(4)]
        vtok = sb("vtok", [128, 4, 4, 128], BF16); vtokB = [[Buf() for _ in range(4)] for _ in range(4)]
        ztok = sb("ztok", [128, 4, T], BF16); ztokB = [Buf() for _ in range(4)]
        bft = sb("bft", [128, T], BF16); bftB = Buf("bft")
        qx = sb("qx", [128, 8, T], BF16); mqB = [Buf() for _ in range(4)]
        mkT = sb("mkT", [128, 4, T], BF16); mkB = [Buf() for _ in range(4)]
        vt = sb("vt", [128, 4, 512], BF16); vtB = Buf("vt")
        PTs = [sb(f"PT{k}", [128, T], BF16) for k in range(4)]; PTB = [Buf(f"PT{k}") for k in range(4)]
        rs, rsB = tmps[2], tmpB[2]
        cs = sb("cs", [128, 2, T]); csB = Buf("cs")
        gmt = sb("gmt", [128, 2, 2, 256]); gmtB = Buf("gmt")
        gmm = sb("gmm", [128, 256]); gmmB = Buf("gmm")
        bsel = sb("bsel", [128, 256]); bselB = Buf("bsel")
        top8 = sb("top8", [128, 8, 8]); top8B = Buf("top8")
        km12 = sb("km12", [128, 2]); km12B = Buf("km12")
        gat = sb("gat", [128, 4, 8]); gatB = Buf("gat")
        gg = sb("gg", [128, 4, 4]); ggB = Buf("gg")
        beta = sb("beta", [128, 4, 4]); betaB = Buf("beta")
        gtmp = sb("gtmp", [128, 4, 4]); gtmpB = Buf("gtmp")
        egs = sb("egs", [128, 12]); egsB = Buf("egs")
        bge = sb("bge", [128, 4]); bgeB = Buf("bge")
        ssq = sb("ssq", [128, 4]); ssqB = [Buf() for _ in range(4)]
        rsd = sb("rsd", [128, 4]); rsdB = [Buf() for _ in range(4)]
        mixT = sb("mixT", [128, 8, T], BF16); mixB = [Buf(f"mix{c}") for c in range(8)]
        Kp = [[sb(f"Kp{a}_{k}", [128, 512], BF16) for k in range(2)] for a in range(2)]
        KpB = [[Buf() for k in range(2)] for a in range(2)]
        Vp = [[sb(f"Vp{a}_{k}", [128, 4, 128], BF16) for k in range(2)] for a in range(2)]
        VpB = [[Buf() for k in range(2)] for a in range(2)]
        KdB = [Buf(f"Kd{h}") for h in range(8)]
        VdB = Buf("Vd")
        _gs = sb("Gs", [128, 128]); _gsB = Buf("Gs")
        Gs = [_gs] * 4; GsB = [_gsB] * 4
        Dst = [sb(f"Dst{h}", [128, 128]) for h in range(4)]; DstB = [Buf() for _ in range(4)]
        DTi = [sb(f"DTi{h}", [128, 128]) for h in range(4)]; DTiB = [Buf() for _ in range(4)]
        Ap = [sb(f"Ap{h}", [128, 128]) for h in range(4)]; ApB = [Buf() for _ in range(4)]
        Bp = [sb(f"Bp{h}", [128, 128]) for h in range(4)]; BpB = [Buf() for _ in range(4)]
        inT = [sb(f"inT{h}", [128, 128], BF16) for h in range(4)]; inTB = [Buf() for _ in range(4)]
        Xs = [sb(f"X{h}", [128, 256]) for h in range(4)]; XsB = [Buf() for _ in range(4)]
        kdec = [sb(f"kdec{h}", [128, 128], BF16) for h in range(4)]; kdecB = [Buf() for _ in range(4)]
        wT = [sb(f"wT{h}", [128, 128], BF16) for h in range(4)]; wTB = [Buf() for _ in range(4)]
        vnew = [sb(f"vnew{h}", [128, 128], BF16) for h in range(4)]; vnewB = [Buf() for _ in range(4)]
        ofp = [sb(f"ofp{h}", [128, 128]) for h in range(4)]; ofpB = [Buf() for _ in range(4)]
        otm = [sb(f"otm{h}", [128, 128]) for h in range(4)]; otmB = [Buf() for _ in range(4)]
        ogb = [sb(f"ogb{h}", [128, 128], BF16) for h in range(4)]; ogbB = [Buf() for _ in range(4)]

        ident_bf = cb[:, CB_ID:CB_ID + 128]
        ones_bf = cb[:, CB_ONES:CB_ONES + 128]
        ones_bd = cb[:, CB_ONESBD:CB_ONESBD + 128]
        maskS = cb[:, CB_MS:CB_MS + 128]
        maskIT = cb[:, CB_MIT:CB_MIT + 128]
        ident_f = cf[:, CF_ID:CF_ID + 128]
        Ui_f = cf[:, CF_UI:CF_UI + 128]
        Lst_f = cf[:, CF_LST:CF_LST + 128]
        ones_f = cf[:, CF_ONES:CF_ONES + 128]
        Rm_f = cf[:, CF_RM:CF_RM + 128]

        def cmask(jj):
            return cb[:, CB_CM + jj * 512:CB_CM + (jj + 1) * 512]

        def dump(name, ap, shape, bufs):
            if not debug:
                return
            d = nc.dram_tensor("dbg_" + name, list(shape), ap.dtype, kind="ExternalOutput").ap()
            dbg_outs[name] = d
            dma(SP, dsem("dbg"), d, ap, rd=bufs)

        dma(SP, dsem("c0"), par[:], params_d[:, :], wr=[parB])
        dma(SP, dsem("c1"), cf[:], cf_d[:, :], wr=[cB])
        for c0 in range(0, NCB, 1024):
            c1 = min(NCB, c0 + 1024)
            dma(POOL, dsem("c2"), cb[:, c0:c1], cb_d[:, c0:c1], wr=[cB])
        op(DVE, lambda: nc.vector.memset(hist[:], 0.0), wr=histB)
        op(DVE, lambda: nc.vector.memset(S_f[:], 0.0), wr=SfB)
        op(DVE, lambda: nc.vector.memset(S_b[:], 0.0), wr=SbB)
        op(DVE, lambda: nc.vector.memset(kmT[:], 0.0), wr=[kmB])
        op(DVE, lambda: nc.vector.memset(qx[:], 0.0), wr=mqB)
        op(ACT, lambda: nc.scalar.activation(out=par[:, P_ALOG:P_ALOG + 16], in_=par[:, P_ALOG:P_ALOG + 16], func=AF.Exp), rd=[parB], wr=[parB])
        op(DVE, lambda: nc.vector.tensor_scalar_mul(out=par[:, P_ALOG:P_ALOG + 16], in0=par[:, P_ALOG:P_ALOG + 16], scalar1=-1.0), rd=[parB], wr=[parB])

        WMAP = {"w1g": w1g, "w1u": w1u, "w1d": w1d, "w2g": w2g, "w2u": w2u, "w2d": w2d, "win": win, "wout": wout}

        def ffn_slabs(n):
            wg, wu, wd = f"w{n}g", f"w{n}u", f"w{n}d"
            for fh in range(2):
                f0 = fh * 1408
                for (n0, ncl) in ((0, 512), (512, 512), (1024, 384)):
                    yield (wg, 0, 8, f0 + n0, ncl)
                    yield (wu, 0, 8, f0 + n0, ncl)
                for dg in range(4):
                    yield (wd, f0, 11, dg * 256, 256)

        def inproj_slabs():
            for c0 in (0, 512, 1024, 1536):
                yield ("win", 0, 8, c0, 512)
            yield ("win", 0, 8, 2048, 8)
            for c0 in (2056, 2568, 3080):
                yield ("win", 0, 8, c0, 512)

        def outproj_slabs():
            for c0 in (0, 512):
                yield ("wout", 0, 8, c0, 512)

        record = worder is None
        wrec = []
        wq = []
        wst = {"n": 0, "next": 0}

        def w_load(k, spec):
            (wn, r0, kc, c0, ncl) = spec
            w = WMAP[wn]
            view = wslots[k][:, 0:kc * ncl].rearrange("p (c n) -> p c n", c=kc)
            src = w[r0:r0 + kc * 128, c0:c0 + ncl].rearrange("(c p) n -> p c n", p=128)
            dma(POOL, dsem(f"w{k}"), view, src, wr=[wslotB[k]])
            return (view, wslotB[k], k)

        def w_issue(k):
            if record or wst["next"] >= len(worder):
                return
            spec = worder[wst["next"]]
            wst["next"] += 1
            wq.append((spec, w_load(k, spec)))

        if not record:
            for k in range(NSLOT):
                w_issue(k)

        def init_kv():
            for h in range(8):
                for c0 in (0, 2048):
                    dma(POOL, dsem(f"kinit{h}"), Kd[h, 64:128, c0:c0 + 2048], koh_d[:, c0:c0 + 2048], wr=[KdB[h]])
            for t0 in range(0, S, 128):
                dma(POOL, dsem("vinit"), Vd[t0:t0 + 128, :, :], von_d[:, :, :], wr=[VdB])


        def wget(spec):
            if record:
                wrec.append(spec)
                k = wst["n"] % NSLOT
                wst["n"] += 1
                return w_load(k, spec)
            sp, v = wq.pop(0)
            assert sp == spec, (sp, spec)
            return v

        def wdone(k):
            w_issue(k)

        def mm(out, lhsT, rhs, start, stop, rd, wr, **kw):
            return op(PE, lambda: nc.tensor.matmul(out, lhsT=lhsT, rhs=rhs, start=start, stop=stop, **kw), rd=rd, wr=wr)

        def act(out, in_, func, rd, wr, **kw):
            return op(ACT, lambda: nc.scalar.activation(out=out, in_=in_, func=func, **kw), rd=rd, wr=wr)

        def sigmoid_inplace(buf_ap, in_ap, rd, B):
            act(buf_ap, in_ap, AF.Exp, rd=rd, wr=[B], scale=-1.0)
            act(buf_ap, buf_ap, AF.Ln, rd=[B], wr=[B], bias=1.0)
            act(buf_ap, buf_ap, AF.Exp, rd=[B], wr=[B], scale=-1.0)

        def rstd_from(out_ap, in_ap, scale, rd, B):
            act(out_ap, in_ap, AF.Ln, rd=rd, wr=[B], scale=scale, bias=EPS)
            act(out_ap, out_ap, AF.Exp, rd=[B], wr=[B], scale=-0.5)

        def barrier():
            engs = (PE, ACT, DVE, POOL)
            for a in engs:
                for b in engs:
                    if a is b or b.n == 0:
                        continue
                    if a.waited.get(b, 0) >= b.n:
                        continue
                    a.eng.wait_ge(b.sem, b.n)
                    a.waited[b] = b.n

        def norm_h(gcol, X, bank, mixer):
            xs, xsB = X.xs, X.xsB
            hT, hB = (hTm, hBm) if mixer else (hTf, hBf)
            talloc = tmp if mixer else tmpF
            pb, pB = bank()
            if mixer:
                for c in range(8):
                    act(bft2[:, :], xs[:, c, :], AF.Square, rd=[xsB[c]], wr=[bft2B])
                    mm(pb[:, :], ones_bf, bft2[:, :], c == 0, c == 7, rd=[bft2B, cB], wr=[pB])
                    if c % 2 == 1:
                        yield
            else:
                for c in range(8):
                    act(actT[:, c, :], xs[:, c, :], AF.Square, rd=[xsB[c]], wr=[actB[c]])
                yield
                for c in range(8):
                    mm(pb[:, :], ones_bf, actT[:, c, :], c == 0, c == 7, rd=[actB[c], cB], wr=[pB])
            r, rB = talloc()
            rstd_from(r[:, :], pb[:, :], 1.0 / D, [pB], rB)
            yield
            for c in range(8):
                op(DVE, lambda: nc.vector.scalar_tensor_tensor(out=hT[:, c, :], in0=xs[:, c, :], scalar=par[:, gcol + c:gcol + c + 1], in1=r[:, :], op0=ALU.mult, op1=ALU.mult),
                   rd=[xsB[c], rB, parB], wr=[hB[c]])
                if c % 4 == 3:
                    yield

        def ffn(n, gcol, X, bank=P_ALL):
            xs, xsB = X.xs, X.xsB
            hT, hB = hTf, hBf
            sl_ = ffn_slabs(n)
            yield from norm_h(gcol, X, bank, False)
            for fh in range(2):
                for (n0, ncl) in ((0, 512), (512, 512), (1024, 384)):
                    gv, gB, gk = wget(next(sl_))
                    uv, uB, uk = wget(next(sl_))
                    for j in range(ncl // 128):
                        fl = n0 // 128 + j
                        pg, pgB = bank()
                        pu, puB = bank()
                        for c in range(8):
                            mm(pg[:, :], gv[:, c, j * 128:(j + 1) * 128], hT[:, c, :], c == 0, c == 7, rd=[gB, hB[c]], wr=[pgB])
                        for c in range(8):
                            mm(pu[:, :], uv[:, c, j * 128:(j + 1) * 128], hT[:, c, :], c == 0, c == 7, rd=[uB, hB[c]], wr=[puB])
                        e, eB = tmpF()
                        sigmoid_inplace(e[:, :], pg[:, :], [pgB], eB)
                        t, tB = tmpF()
                        op(DVE, lambda: nc.vector.tensor_tensor(out=t[:, :], in0=pg[:, :], in1=e[:, :], op=ALU.mult), rd=[pgB, eB], wr=[tB])
                        op(DVE, lambda: nc.vector.tensor_tensor(out=actT[:, fl, :], in0=pu[:, :], in1=t[:, :], op=ALU.mult), rd=[puB, tB], wr=[actB[fl]])
                        yield
                    wdone(gk)
                    wdone(uk)
                for dg in range(4):
                    wv, wB, wk = wget(next(sl_))
                    for dd in range(2):
                        d = dg * 2 + dd
                        po, poB = bank()
                        for f in range(11):
                            mm(po[:, :], wv[:, f, dd * 128:(dd + 1) * 128], actT[:, f, :], f == 0, f == 10, rd=[wB, actB[f]], wr=[poB])
                        op(DVE, lambda: nc.vector.scalar_tensor_tensor(out=xs[:, d, :], in0=po[:, :], scalar=0.5, in1=xs[:, d, :], op0=ALU.mult, op1=ALU.add),
                           rd=[poB, xsB[d]], wr=[xsB[d]])
                        yield
                    wdone(wk)

        def gdn_qkv(bank, sl_):
            hT, hB = hTm, hBm
            for which in range(3):
                wv, wB, wk = wget(next(sl_))
                for hc in range(4):
                    c12 = which * 4 + hc
                    pb, pB = bank()
                    for c in range(8):
                        mm(pb[:, :], wv[:, c, hc * 128:(hc + 1) * 128], hT[:, c, :], c == 0, c == 7, rd=[wB, hB[c]], wr=[pB])
                    if hc == 3:
                        wdone(wk)
                    op(DVE, lambda: nc.vector.tensor_copy(pcx[:, 0:3], hist[:, c12, :]), rd=[histB[c12]], wr=[pcxB])
                    act(pcx[:, 3:T + 3], pb[:, :], AF.Copy, rd=[pB], wr=[pcxB])
                    op(DVE, lambda: nc.vector.tensor_copy(hist[:, c12, :], pcx[:, T:T + 3]), rd=[pcxB], wr=[histB[c12]])
                    ca, caB = tmpA()
                    cw = P_CONV + c12 * 4
                    op(DVE, lambda: nc.vector.tensor_scalar_mul(out=ca[:, :], in0=pcx[:, 0:T], scalar1=par[:, cw:cw + 1]), rd=[pcxB, parB], wr=[caB])
                    for k in range(1, 4):
                        op(DVE, lambda: nc.vector.scalar_tensor_tensor(out=ca[:, :], in0=pcx[:, k:k + T], scalar=par[:, cw + k:cw + k + 1], in1=ca[:, :], op0=ALU.mult, op1=ALU.add),
                           rd=[pcxB, parB, caB], wr=[caB])
                    e, eB = tmpA()
                    sigmoid_inplace(e[:, :], ca[:, :], [caB], eB)
                    if which == 2:
                        op(DVE, lambda: nc.vector.tensor_tensor(out=bft[:, :], in0=ca[:, :], in1=e[:, :], op=ALU.mult), rd=[caB, eB], wr=[bftB])
                        yield
                        for blk in range(4):
                            pt, ptB = bbank(bank)
                            op(PE, lambda: nc.tensor.transpose(out=pt[:, 0:128], in_=bft[:, blk * 128:(blk + 1) * 128], identity=ident_bf), rd=[bftB, cB], wr=[ptB])
                            act(vtok[:, blk, hc, :], pt[:, 0:128], AF.Copy, rd=[ptB], wr=[vtokB[blk][hc]])
                        yield
                        continue
                    op(DVE, lambda: nc.vector.tensor_tensor(out=ca[:, :], in0=ca[:, :], in1=e[:, :], op=ALU.mult), rd=[caB, eB], wr=[caB])
                    act(bft[:, :], ca[:, :], AF.Square, rd=[caB], wr=[bftB])
                    yield
                    p2, p2B = bank()
                    mm(p2[:, :], ones_bf, bft[:, :], True, True, rd=[bftB, cB], wr=[p2B])
                    rstd_from(e[:, :], p2[:, :], 1.0, [p2B], eB)
                    dst, dB = (qT_b, qTB) if which == 0 else (kT_b, kTB)
                    sc = (128.0 ** -0.5) if which == 0 else 1.0
                    op(DVE, lambda: nc.vector.scalar_tensor_tensor(out=dst[:, hc, :], in0=ca[:, :], scalar=sc, in1=e[:, :], op0=ALU.mult, op1=ALU.mult), rd=[caB, eB], wr=[dB[hc]])
                    yield
                    if which == 1:
                        for blk in range(4):
                            pt, ptB = bbank(bank)
                            op(PE, lambda: nc.tensor.transpose(out=pt[:, 0:128], in_=kT_b[:, hc, blk * 128:(blk + 1) * 128], identity=ident_bf), rd=[kTB[hc], cB], wr=[ptB])
                            act(ktok[:, blk, hc, :], pt[:, 0:128], AF.Copy, rd=[ptB], wr=[ktokB[blk][hc]])
                        yield

        def gdn_z_gates(bank, sl_):
            hT, hB = hTm, hBm
            wv, wB, wk = wget(next(sl_))
            for blk in range(4):
                pb, pB = bank()
                for c in range(8):
                    mm(pb[:, :], hT[:, c, blk * 128:(blk + 1) * 128], wv[:, c, :], c == 0, c == 7, rd=[wB, hB[c]], wr=[pB])
                e, eB = tmpB_()
                sigmoid_inplace(e[:, :], pb[:, :], [pB], eB)
                op(DVE, lambda: nc.vector.tensor_tensor(out=ztok[:, blk, :], in0=pb[:, :], in1=e[:, :], op=ALU.mult), rd=[pB, eB], wr=[ztokB[blk]])
                yield
            wdone(wk)
            wv, wB, wk = wget(next(sl_))
            pb, pB = bank()
            for blk in range(4):
                for c in range(8):
                    mm(pb[:, blk * 8:(blk + 1) * 8], hT[:, c, blk * 128:(blk + 1) * 128], wv[:, c, :], c == 0, c == 7, rd=[wB, hB[c]], wr=[pB])
            wdone(wk)
            act(gat[:, :, :], pb[:, 0:32].rearrange("p (b k) -> p b k", b=4), AF.Copy, rd=[pB], wr=[gatB])
            op(DVE, lambda: nc.vector.tensor_tensor(out=gtmp[:, :, :], in0=gat[:, :, 0:4], in1=par[:, P_DTB:P_DTB + 16].rearrange("p (b k) -> p b k", b=4), op=ALU.add), rd=[gatB, parB], wr=[gtmpB])
            act(gtmp[:, :, :], gtmp[:, :, :], AF.Exp, rd=[gtmpB], wr=[gtmpB])
            act(gtmp[:, :, :], gtmp[:, :, :], AF.Ln, rd=[gtmpB], wr=[gtmpB], bias=1.0)
            op(DVE, lambda: nc.vector.tensor_tensor(out=gg[:, :, :], in0=gtmp[:, :, :], in1=par[:, P_ALOG:P_ALOG + 16].rearrange("p (b k) -> p b k", b=4), op=ALU.mult), rd=[gtmpB, parB], wr=[ggB])
            sigmoid_inplace(beta[:, :, :], gat[:, :, 4:8], [gatB], betaB)
            yield

        def moba_qkv(i, bank, sl_):
            hT, hB = hTm, hBm
            dma(SP, dsem("cs"), cs[:], cs_d[:, :, i * T:(i + 1) * T], wr=[csB])
            dma(SP, dsem("gm"), gmt[:], gm_d[:, 2 * i:2 * i + 2, :, :], wr=[gmtB])
            for which in range(2):
                wv, wB, wk = wget(next(sl_))
                for c in range(4):
                    pb, pB = bank()
                    for cc in range(8):
                        mm(pb[:, :], wv[:, cc, c * 128:(c + 1) * 128], hT[:, cc, :], cc == 0, cc == 7, rd=[wB, hB[cc]], wr=[pB])
                    if c == 3:
                        wdone(wk)
                    xf, xfB = tmpB_()
                    act(xf[:, :], pb[:, :], AF.Copy, rd=[pB], wr=[xfB])
                    act(bft2[:, :], pb[:, :], AF.Square, rd=[pB], wr=[bft2B])
                    yield
                    p2, p2B = bank()
                    mm(p2[:, :], ones_bd, bft2[:, :], True, True, rd=[bft2B, cB], wr=[p2B])
                    r, rB = tmpB_()
                    rstd_from(r[:, :], p2[:, :], 1.0 / 64.0, [p2B], rB)
                    gc = P_QG if which == 0 else P_KG
                    op(DVE, lambda: nc.vector.scalar_tensor_tensor(out=xf[:, :], in0=xf[:, :], scalar=par[:, gc:gc + 1], in1=r[:, :], op0=ALU.mult, op1=ALU.mult), rd=[xfB, rB, parB], wr=[xfB])
                    yield
                    p3, p3B = bank()
                    mm(p3[:, :], Rm_f, xf[:, :], True, True, rd=[xfB, cB], wr=[p3B])
                    op(DVE, lambda: nc.vector.tensor_tensor(out=r[:, :], in0=p3[:, :], in1=cs[:, 1, :], op=ALU.mult), rd=[p3B, csB], wr=[rB])
                    op(DVE, lambda: nc.vector.tensor_tensor(out=xf[:, :], in0=xf[:, :], in1=cs[:, 0, :], op=ALU.mult), rd=[xfB, csB], wr=[xfB])
                    op(DVE, lambda: nc.vector.tensor_tensor(out=xf[:, :], in0=xf[:, :], in1=r[:, :], op=ALU.add), rd=[xfB, rB], wr=[xfB])
                    if which == 0:
                        act(qx[0:64, 2 * c, :], xf[0:64, :], AF.Copy, rd=[xfB], wr=[mqB[c]])
                        act(qx[0:64, 2 * c + 1, :], xf[64:128, :], AF.Copy, rd=[xfB, mqB[c]], wr=[mqB[c]])
                    else:
                        act(mkT[:, c, :], xf[:, :], AF.Copy, rd=[xfB], wr=[mkB[c]])
                        for hp in range(2):
                            dma(SP, dsem(f"kst{2 * c + hp}"), Kd[2 * c + hp, 0:64, i * T:(i + 1) * T], mkT[hp * 64:(hp + 1) * 64, c, :], rd=[mkB[c]], wr=[KdB[2 * c + hp]])
                        op(DVE, lambda: nc.vector.tensor_reduce(out=km12[:, 0:1], in_=xf[:, 0:256], axis=AX.X, op=ALU.add), rd=[xfB], wr=[km12B])
                        op(DVE, lambda: nc.vector.tensor_reduce(out=km12[:, 1:2], in_=xf[:, 256:512], axis=AX.X, op=ALU.add), rd=[xfB, km12B], wr=[km12B])
                        act(kmT[0:64, 2 * c, 2 * i:2 * i + 2], km12[0:64, 0:2], AF.Copy, rd=[km12B], wr=[kmB], scale=1.0 / 256.0)
                        act(kmT[0:64, 2 * c + 1, 2 * i:2 * i + 2], km12[64:128, 0:2], AF.Copy, rd=[km12B, kmB], wr=[kmB], scale=1.0 / 256.0)
                    yield
            wv, wB, wk = wget(next(sl_))
            for sub in range(4):
                pb, pB = bank()
                for cc in range(8):
                    mm(pb[:, :], hT[:, cc, sub * 128:(sub + 1) * 128], wv[:, cc, :], cc == 0, cc == 7, rd=[wB, hB[cc]], wr=[pB])
                act(vt[:, sub, :], pb[:, :], AF.Copy, rd=[pB], wr=[vtB])
                yield
            wdone(wk)
            for h in range(8):
                off = 0 if h % 2 == 0 else 64
                dma(SP, dsem("vst"), Vd[i * T:(i + 1) * T, h, off:off + 64].rearrange("(s p) f -> p s f", p=128), vt[:, :, h * 64:(h + 1) * 64], rd=[vtB], wr=[VdB])

        def moba_gate(i, bank):
            for sub in range(4):
                ow = sub // 2
                pb, pB = bank()
                for h in range(8):
                    c, hp = h // 2, h % 2
                    mm(pb[:, h * 32:(h + 1) * 32], qx[:, h, sub * 128:(sub + 1) * 128], kmT[:, h, :], True, True,
                       rd=[mqB[c], kmB], wr=[pB])
                op(DVE, lambda: nc.vector.tensor_tensor(out=gmm[:, :], in0=pb[:, 0:256], in1=gmt[:, ow, 0, :], op=ALU.add), rd=[pB, gmtB], wr=[gmmB])
                for h in range(8):
                    op(DVE, lambda: nc.vector.max(out=top8[:, h, :], in_=gmm[:, h * 32:(h + 1) * 32]), rd=[gmmB], wr=[top8B])
                for h in range(8):
                    op(DVE, lambda: nc.vector.tensor_scalar(out=bsel[:, h * 32:(h + 1) * 32], in0=gmm[:, h * 32:(h + 1) * 32], scalar1=top8[:, h, 2:3], scalar2=1.0, op0=ALU.is_ge, op1=ALU.subtract),
                       rd=[gmmB, top8B], wr=[bselB])
                op(DVE, lambda: nc.vector.scalar_tensor_tensor(out=bsel[:, :], in0=bsel[:, :], scalar=-NEG, in1=gmt[:, ow, 1, :], op0=ALU.mult, op1=ALU.mult), rd=[bselB, gmtB], wr=[bselB])
                yield
                for g2 in range(2):
                    pt, ptB = bank()
                    op(PE, lambda: nc.tensor.transpose(out=pt[:, 0:128], in_=bsel[:, g2 * 128:(g2 + 1) * 128], identity=ident_f), rd=[bselB, cB], wr=[ptB])
                    for g in range(4):
                        hh = g2 * 4 + g
                        act(qx[64:80, hh, sub * 128:(sub + 1) * 128], pt[g * 32:g * 32 + 16, 0:128], AF.Copy, rd=[ptB, mqB[hh // 2]], wr=[mqB[hh // 2]])
                yield

        def moba_stream(i, sid, heads):
            nj = 4 * i + 4
            npc = (nj + 3) // 4
            ps_, psB = banks[4 + sid]
            po, poB = banks[6 + sid]
            ring = {"k": 0}
            pieces = [(h, pc) for h in heads for pc in range(npc)]

            def load(idx):
                h, pc = pieces[idx]
                k = idx % 2
                nk = min(4, nj - pc * 4)
                dma(SP, dsem(f"kl{sid}{k}"), Kp[sid][k][:, 0:nk * 128], Kd[h, :, pc * 512:pc * 512 + nk * 128], rd=[KdB[h]], wr=[KpB[sid][k]])
                dma(SP, dsem(f"vl{sid}{k}"), Vp[sid][k][:, 0:nk, :], Vd[pc * 512:pc * 512 + nk * 128, h, :].rearrange("(j p) f -> p j f", p=128), rd=[VdB], wr=[VpB[sid][k]])

            load(0)
            pend = None
            ptk = 0
            for idx, (h, pc) in enumerate(pieces):
                if pend is not None:
                    pend()
                    pend = None
                if idx + 1 < len(pieces):
                    load(idx + 1)
                k = idx % 2
                c, hp = h // 2, h % 2
                nk = min(4, nj - pc * 4)
                for jj in range(nk):
                    j = pc * 4 + jj
                    diag = j >= 4 * i
                    mm(ps_[:, :], Kp[sid][k][:, jj * 128:(jj + 1) * 128], qx[:, h, :], True, not diag, rd=[KpB[sid][k], mqB[c]], wr=[psB])
                    if diag:
                        mm(ps_[:, :], ident_bf, cmask(j - 4 * i), False, True, rd=[cB], wr=[psB])
                    if pend is not None:
                        pend()
                        pend = None
                    PT, PB = PTs[sid * 2 + ptk % 2], PTB[sid * 2 + ptk % 2]
                    ptk += 1
                    act(PT[:, :], ps_[:, :], AF.Exp, rd=[psB], wr=[PB], scale=0.125)

                    def pv(PT=PT, PB=PB, j=j, jj=jj, k=k):
                        mm(po[:, :], Vp[sid][k][:, jj, :], PT[:, :], j == 0, j == nj - 1, rd=[VpB[sid][k], PB], wr=[poB])
                    pend = pv
                    if j == nj - 1:
                        pend()
                        pend = None
                        osl = slice(hp * 64, (hp + 1) * 64)
                        ssl = slice((1 - hp) * 64, (2 - hp) * 64)
                        act(rs[osl, :], po[ssl, :], AF.Ln, rd=[poB], wr=[rsB])
                        act(rs[osl, :], rs[osl, :], AF.Exp, rd=[rsB], wr=[rsB], scale=-1.0)
                        op(DVE, lambda: nc.vector.tensor_tensor(out=mixT[osl, 4 + c, :], in0=po[osl, :], in1=rs[osl, :], op=ALU.mult), rd=[poB, rsB, mixB[4 + c]], wr=[mixB[4 + c]])
                    yield

        def gdn_block(blk, gb=P_GDN):
            bs = slice(blk * 128, (blk + 1) * 128)
            pb, pB = gb()
            mm(pb[:, 0:4], Ui_f, gg[:, blk, :], True, True, rd=[ggB, cB], wr=[pB])
            mm(pb[:, 4:8], Lst_f, gg[:, blk, :], True, True, rd=[ggB, cB], wr=[pB])
            mm(pb[:, 8:12], ones_f, gg[:, blk, :], True, True, rd=[ggB, cB], wr=[pB])
            act(egs[:, :], pb[:, 0:12], AF.Exp, rd=[pB], wr=[egsB])
            op(DVE, lambda: nc.vector.tensor_tensor(out=bge[:, :], in0=beta[:, blk, :], in1=egs[:, 0:4], op=ALU.mult), rd=[betaB, egsB], wr=[bgeB])
            yield

            def sD(h):
                op(DVE, lambda: nc.vector.tensor_scalar_mul(out=Gs[h][:, :], in0=Lst_f, scalar1=gg[:, blk, h:h + 1]), rd=[cB, ggB], wr=[GsB[h]])
                p1, p1B = gb()
                mm(p1[:, 0:128], Ui_f, Gs[h][:, :], True, False, rd=[cB, GsB[h]], wr=[p1B])
                mm(p1[:, 0:128], ident_bf, maskS, False, True, rd=[cB], wr=[p1B])
                mm(p1[:, 128:256], Gs[h][:, :], Ui_f, True, False, rd=[cB, GsB[h]], wr=[p1B])
                mm(p1[:, 128:256], ident_bf, maskIT, False, True, rd=[cB], wr=[p1B])
                act(Dst[h][:, :], p1[:, 0:128], AF.Exp, rd=[p1B], wr=[DstB[h]])
                act(DTi[h][:, :], p1[:, 128:256], AF.Exp, rd=[p1B], wr=[DTiB[h]])

            def sG(h):
                p2, p2B = gb()
                mm(p2[:, 0:128], kT_b[:, h, bs], kT_b[:, h, bs], True, True, rd=[kTB[h]], wr=[p2B])
                mm(p2[:, 128:256], kT_b[:, h, bs], qT_b[:, h, bs], True, True, rd=[kTB[h], qTB[h]], wr=[p2B])
                op(DVE, lambda: nc.vector.scalar_tensor_tensor(out=Ap[h][:, :], in0=p2[:, 0:128], scalar=beta[:, blk, h:h + 1], in1=Dst[h][:, :], op0=ALU.mult, op1=ALU.mult),
                   rd=[p2B, betaB, DstB[h]], wr=[ApB[h]])
                op(DVE, lambda: nc.vector.tensor_tensor(out=inT[h][:, :], in0=p2[:, 128:256], in1=DTi[h][:, :], op=ALU.mult), rd=[p2B, DTiB[h]], wr=[inTB[h]])
                op(DVE, lambda: nc.vector.tensor_scalar_mul(out=Xs[h][:, 0:128], in0=vtok[:, blk, h, :], scalar1=beta[:, blk, h:h + 1]), rd=[vtokB[blk][h], betaB], wr=[XsB[h]])
                op(DVE, lambda: nc.vector.tensor_scalar_mul(out=Xs[h][:, 128:256], in0=ktok[:, blk, h, :], scalar1=bge[:, h:h + 1]), rd=[ktokB[blk][h], bgeB, XsB[h]], wr=[XsB[h]])
                op(DVE, lambda: nc.vector.tensor_scalar_mul(out=kdec[h][:, :], in0=ktok[:, blk, h, :], scalar1=egs[:, 4 + h:5 + h]), rd=[ktokB[blk][h], egsB], wr=[kdecB[h]])

            def sT(h):
                pt, ptB = gb()
                op(PE, lambda: nc.tensor.transpose(out=pt[:, 0:128], in_=Ap[h][:, :], identity=ident_f), rd=[ApB[h], cB], wr=[ptB])
                act(Bp[h][:, :], pt[:, 0:128], AF.Copy, rd=[ptB], wr=[BpB[h]])

            def mk_solve(s):
                def f(h):
                    pk, pkB = gb()
                    mm(pk[:, 0:256], Bp[h][:, :], Xs[h][:, :], True, True, rd=[BpB[h], XsB[h]], wr=[pkB])
                    if s < 6:
                        mm(pk[:, 256:384], Bp[h][:, :], Ap[h][:, :], True, True, rd=[BpB[h], ApB[h]], wr=[pkB])
                        mm(pk[:, 384:512], Ap[h][:, :], Bp[h][:, :], True, True, rd=[BpB[h], ApB[h]], wr=[pkB])
                    if s == 0:
                        op(DVE, lambda: nc.vector.scalar_tensor_tensor(out=Xs[h][:, :], in0=pk[:, 0:256], scalar=-1.0, in1=Xs[h][:, :], op0=ALU.mult, op1=ALU.add),
                           rd=[pkB, XsB[h]], wr=[XsB[h]])
                    else:
                        op(DVE, lambda: nc.vector.tensor_tensor(out=Xs[h][:, :], in0=pk[:, 0:256], in1=Xs[h][:, :], op=ALU.add),
                           rd=[pkB, XsB[h]], wr=[XsB[h]])
                    if s < 6:
                        act(Ap[h][:, :], pk[:, 256:384], AF.Copy, rd=[pkB], wr=[ApB[h]])
                        act(Bp[h][:, :], pk[:, 384:512], AF.Copy, rd=[pkB], wr=[BpB[h]])
                return f

            def sW(h):
                pt, ptB = gb()
                op(PE, lambda: nc.tensor.transpose(out=pt[:, 0:128], in_=Xs[h][:, 128:256], identity=ident_f), rd=[XsB[h], cB], wr=[ptB])
                act(wT[h][:, :], pt[:, 0:128], AF.Copy, rd=[ptB], wr=[wTB[h]])

            def sV(h):
                pv, pvB = gb()
                mm(pv[:, 0:128], wT[h][:, :], S_b[:, h, :], True, True, rd=[wTB[h], SbB[h]], wr=[pvB])
                op(DVE, lambda: nc.vector.scalar_tensor_tensor(out=vnew[h][:, :], in0=pv[:, 0:128], scalar=-1.0, in1=Xs[h][:, 0:128], op0=ALU.mult, op1=ALU.add), rd=[pvB, XsB[h]], wr=[vnewB[h]])

            def sO(h):
                po, poB = gb()
                mm(po[:, 0:128], qT_b[:, h, bs], S_b[:, h, :], True, True, rd=[qTB[h], SbB[h]], wr=[poB])
                mm(po[:, 128:256], inT[h][:, :], vnew[h][:, :], True, True, rd=[inTB[h], vnewB[h]], wr=[poB])
                mm(po[:, 256:384], kdec[h][:, :], vnew[h][:, :], True, True, rd=[kdecB[h], vnewB[h]], wr=[poB])
                act(otm[h][:, :], po[:, 0:128], AF.Copy, rd=[poB, egsB], wr=[otmB[h]], scale=egs[:, h:h + 1])
                op(DVE, lambda: nc.vector.scalar_tensor_tensor(out=S_f[:, h, :], in0=S_f[:, h, :], scalar=egs[:, 8 + h:9 + h], in1=po[:, 256:384], op0=ALU.mult, op1=ALU.add),
                   rd=[SfB[h], egsB, poB], wr=[SfB[h]])
                op(DVE, lambda: nc.vector.tensor_tensor(out=ofp[h][:, :], in0=po[:, 128:256], in1=otm[h][:, :], op=ALU.add), rd=[poB, otmB[h]], wr=[ofpB[h]])
                act(S_b[:, h, :], S_f[:, h, :], AF.Copy, rd=[SfB[h]], wr=[SbB[h]])
                op(DVE, lambda: nc.vector.memset(ssq[:, h:h + 1], 0.0), wr=[ssqB[h]])
                act(otm[h][:, :], ofp[h][:, :], AF.Square, rd=[ofpB[h], ssqB[h]], wr=[otmB[h], ssqB[h]], accum_out=ssq[:, h:h + 1])
                rstd_from(rsd[:, h:h + 1], ssq[:, h:h + 1], 1.0 / 128.0, [ssqB[h]], rsdB[h])
                op(DVE, lambda: nc.vector.scalar_tensor_tensor(out=ofp[h][:, :], in0=ofp[h][:, :], scalar=rsd[:, h:h + 1], in1=par[:, P_ONORM:P_ONORM + 128], op0=ALU.mult, op1=ALU.mult),
                   rd=[ofpB[h], rsdB[h], parB], wr=[ofpB[h]])
                op(DVE, lambda: nc.vector.tensor_tensor(out=ogb[h][:, :], in0=ofp[h][:, :], in1=ztok[:, blk, h * 128:(h + 1) * 128], op=ALU.mult), rd=[ofpB[h], ztokB[blk]], wr=[ogbB[h]])

            def sX(h):
                pt, ptB = bbank(gb)
                op(PE, lambda: nc.tensor.transpose(out=pt[:, 0:128], in_=ogb[h][:, :], identity=ident_bf), rd=[ogbB[h], cB], wr=[ptB])
                act(mixT[:, h, bs], pt[:, 0:128], AF.Copy, rd=[ptB], wr=[mixB[h]])

            steps = [sD, sG, sT] + [mk_solve(s) for s in range(7)] + [sW, sV, sO, sX]
            for st in steps:
                for pair in ((0, 1), (2, 3)):
                    for h in pair:
                        st(h)
                    yield

        def out_proj(X, bank=P_ALL):
            xs, xsB = X.xs, X.xsB
            sl_ = outproj_slabs()
            for dgp in range(2):
                wv, wB, wk = wget(next(sl_))
                for dd in range(4):
                    d = dgp * 4 + dd
                    pb, pB = bank()
                    for m in range(8):
                        mm(pb[:, :], wv[:, m, dd * 128:(dd + 1) * 128], mixT[:, m, :], m == 0, m == 7, rd=[wB, mixB[m]], wr=[pB])
                    op(DVE, lambda: nc.vector.tensor_tensor(out=xs[:, d, :], in0=pb[:, :], in1=xs[:, d, :], op=ALU.add), rd=[pB, xsB[d]], wr=[xsB[d]])
                    yield
                wdone(wk)

        def run(*gens):
            gens = [g for g in gens if g is not None]
            dead = set()
            while len(dead) < len(set(map(id, gens))):
                for g in gens:
                    if id(g) in dead:
                        continue
                    try:
                        next(g)
                    except StopIteration:
                        dead.add(id(g))

        def seq(*fns):
            for f in fns:
                r = f()
                if r is not None:
                    yield from r

        def load_x(i, X):
            dma(SP, dsem(f"xl{i % 2}"), X.xs[:, :, :], xT[:, i * T:(i + 1) * T].rearrange("(c p) t -> p c t", p=128), wr=X.xsB)

        def gdn_all():
            for blk in range(4):
                yield from gdn_block(blk)

        P_F3 = mkpool([0, 1, 2])
        P_I5 = mkpool([3, 4, 5, 6, 7])

        def rr(*gens):
            gens = list(gens)
            while gens:
                for g in list(gens):
                    try:
                        next(g)
                        yield
                    except StopIteration:
                        gens.remove(g)

        def inproj(i, X, bank):
            sl_ = list(inproj_slabs())
            s1 = iter(sl_[0:3])
            s2 = iter(sl_[3:8])
            yield from norm_h(P_MIX, X, bank, True)
            yield from rr(gdn_qkv(bank, s1), seq2(lambda: gdn_z_gates(bank, s2), lambda: moba_qkv(i, bank, s2), lambda: moba_gate(i, bank)))

        def seq2(*fns):
            for f in fns:
                yield from f()

        load_x(0, ctxs[0])
        run(ffn(1, P_N1, ctxs[0]))
        init_kv()
        if debug:
            dump("x1", ctxs[0].xs[:, :, :], [128, 8, T], ctxs[0].xsB)
        run(inproj(0, ctxs[0], P_ALL))
        for i in range(ntiles):
            X = ctxs[i % 2]
            Xn = ctxs[(i + 1) % 2]
            last = i + 1 >= ntiles
            if not last:
                load_x(i + 1, Xn)
            g_gdn = gdn_all()
            run(g_gdn, moba_stream(i, 0, [0, 2, 4, 6]), moba_stream(i, 1, [1, 3, 5, 7]), g_gdn, None if last else ffn(1, P_N1, Xn, P_FFN))
            if debug and i == 0:
                dump("mixT", mixT[:, :, :], [128, 8, T], mixB)
            if last:
                run(out_proj(X))
                run(ffn(2, P_N2, X, P_ALL))
            else:
                g_in = inproj(i + 1, Xn, P_I5)
                run(seq2(lambda: out_proj(X, P_F3), lambda: ffn(2, P_N2, X, P_F3)), g_in, g_in, g_in)
            dma(SP, dsem("st"), yT[:, i * T:(i + 1) * T].rearrange("(c p) t -> p c t", p=128), X.xs[:, :, :], rd=X.xsB)
        for nm, e in dsems.items():
            if nm.startswith("w") and e.n > 0:
                POOL.eng.wait_ge(e.sem, e.n)
        SP.eng.wait_ge(dsems["st"].sem, dsems["st"].n)
        if debug and "dbg" in dsems:
            SP.eng.wait_ge(dsems["dbg"].sem, dsems["dbg"].n)
    return nc, dbg_outs, wrec


def _consts():
    cbv = np.zeros((128, NCB), np.float32)
    cfv = np.zeros((128, NCF), np.float32)
    I = np.eye(128, dtype=np.float32)
    p = np.arange(128)
    cbv[:, CB_ID:CB_ID + 128] = I
    cbv[:, CB_ONES:CB_ONES + 128] = 1.0
    cbv[:, CB_ONESBD:CB_ONESBD + 128] = (p[:, None] // 64 == p[None, :] // 64)
    cbv[:, CB_MS:CB_MS + 128] = np.where(p[:, None] > p[None, :], 0.0, NEG)
    cbv[:, CB_MIT:CB_MIT + 128] = np.where(p[None, :] >= p[:, None], 0.0, NEG)
    q = np.arange(512)
    cm = np.zeros((128, 4, 512), np.float32)
    for jj in range(4):
        cm[:, jj, :] = np.where((jj * 128 + p)[:, None] <= q[None, :], 0.0, NEG)
    cbv[:, CB_CM:CB_CM + 2048] = cm.reshape(128, 2048)
    cfv[:, CF_ID:CF_ID + 128] = I
    cfv[:, CF_UI:CF_UI + 128] = (p[:, None] <= p[None, :])
    cfv[:, CF_LST:CF_LST + 128] = (p[:, None] > p[None, :])
    cfv[:, CF_ONES:CF_ONES + 128] = 1.0
    rm = np.zeros((128, 128), np.float32)
    for pp in range(128):
        d = pp % 64
        if d < 8:
            rm[pp + 8, pp] = -1.0
        elif d < 16:
            rm[pp - 8, pp] = 1.0
    cfv[:, CF_RM:CF_RM + 128] = rm
    half = 8
    inv_freq = np.power(np.float32(500000.0), -np.arange(half, dtype=np.float32) * 2.0 / 16.0).astype(np.float32)
    pos = np.arange(S, dtype=np.float32)
    ang = pos[:, None] * inv_freq[None, :]
    cosv, sinv = np.cos(ang).astype(np.float32), np.sin(ang).astype(np.float32)
    cs = np.zeros((128, 2, S), np.float32)
    cs[:, 0, :] = 1.0
    for pp in range(128):
        d = pp % 64
        if d < 16:
            cs[pp, 0, :] = cosv[:, d % 8]
            cs[pp, 1, :] = sinv[:, d % 8]
    gm = np.zeros((128, 16, 2, 256), np.float32)
    n = np.arange(32)
    for own in range(16):
        row = np.where(n < min(own, 16), 0.0, NEG).astype(np.float32)
        row[16:] = NEG
        pst = (n < own).astype(np.float32)
        pst[16:] = 0.0
        gm[:, own, 0, :] = np.tile(row, 8)[None, :]
        gm[:, own, 1, :] = np.tile(pst, 8)[None, :]
    koh = np.zeros((64, S), np.float32)
    for n in range(16):
        koh[n, n * 256:(n + 1) * 256] = 1.0
    von = np.zeros((128, 8, 128), np.float32)
    for h in range(8):
        if h % 2 == 0:
            von[:, h, 64:128] = 1.0
        else:
            von[:, h, 0:64] = 1.0
    return cbv, cfv, cs, gm, koh, von


def _params(inp):
    pr = np.zeros((128, NPAR), np.float32)

    def chunked(v):
        return np.ascontiguousarray(v.reshape(-1, 128).T)

    pr[:, P_N1:P_N1 + 8] = chunked(inp["ffn1_norm"][0])
    pr[:, P_MIX:P_MIX + 8] = chunked(inp["mix_norm"][0])
    pr[:, P_N2:P_N2 + 8] = chunked(inp["ffn2_norm"][0])
    conv = inp["gdn_conv"][0]
    pr[:, P_CONV:P_CONV + 48] = conv.T.reshape(12, 128, 4).transpose(1, 0, 2).reshape(128, 48)
    pr[:, P_ALOG:P_ALOG + 16] = np.tile(inp["gdn_a_log"][0], 4)[None, :]
    pr[:, P_DTB:P_DTB + 16] = np.tile(inp["gdn_dt_bias"][0], 4)[None, :]
    pr[:, P_ONORM:P_ONORM + 128] = inp["gdn_out_norm"][0][None, :]
    pr[:, P_QG] = np.tile(inp["moba_q_norm"][0], 2)
    pr[:, P_KG] = np.tile(inp["moba_k_norm"][0], 2)
    return pr


_CACHE = {}


def kernel(**inputs):
    inp = {k: np.asarray(v, dtype=np.float32) for k, v in inputs.items()}
    if "nc" not in _CACHE:
        _, _, order = build()
        _CACHE["nc"] = build(worder=order)[0]
        _CACHE["consts"] = _consts()
    nc = _CACHE["nc"]
    cbv, cfv, cs, gm, koh, von = _CACHE["consts"]
    pr = _params(inp)
    shared = {
        "w1g": np.ascontiguousarray(inp["ffn1_w_gate"][0]), "w1u": np.ascontiguousarray(inp["ffn1_w_up"][0]),
        "w1d": np.ascontiguousarray(inp["ffn1_w_down"][0]),
        "w2g": np.ascontiguousarray(inp["ffn2_w_gate"][0]), "w2u": np.ascontiguousarray(inp["ffn2_w_up"][0]),
        "w2d": np.ascontiguousarray(inp["ffn2_w_down"][0]),
        "win": np.ascontiguousarray(inp["w_in"][0]), "wout": np.ascontiguousarray(inp["w_out"][0]),
        "params": pr, "cf": cfv, "cb": cbv, "cossin": cs, "gmk": gm, "koh": koh, "vones": von,
    }
    x = inp["x"]
    in_maps = []
    for b in range(8):
        m = dict(shared)
        m["xT"] = np.ascontiguousarray(x[b].T)
        in_maps.append(m)
    res = run_bass_kernel_spmd(nc, in_maps, core_ids=list(range(8)))
    out = np.empty((8, S, D), np.float32)
    for b in range(8):
        out[b] = res.results[b]["yT"].T
    return out
```

```python
import contextlib
import math
import numpy as np
import concourse.bass as bass
import concourse.mybir as mybir
from concourse.bass_utils import run_bass_kernel_spmd

F32 = mybir.dt.float32
BF16 = mybir.dt.bfloat16
AF = mybir.ActivationFunctionType
ALU = mybir.AluOpType
AX = mybir.AxisListType

S = 4096
D = 1024
T = 512
NT = S // T
DFF = 2816
EPS = 1e-6
NEG = -30000.0
NPAR = 256

CB_ID, CB_ONES, CB_ONESBD, CB_MS, CB_MIT, CB_CM = 0, 128, 256, 384, 512, 640
NCB = 2688
CF_ID, CF_UI, CF_LST, CF_ONES, CF_RM = 0, 128, 256, 384, 512
NCF = 640
P_N1, P_MIX, P_N2, P_CONV, P_ALOG, P_DTB, P_ONORM, P_QG, P_KG = 0, 8, 16, 24, 72, 88, 104, 232, 233


class Eng:
    def __init__(self, name, eng, in_order=False):
        self.name = name
        self.eng = eng
        self.n = 0
        self.sem = None
        self.waited = {}
        self.in_order = in_order


class Buf:
    __slots__ = ("w", "r", "psum", "name")

    def __init__(self, name="", psum=False):
        self.w = None
        self.r = {}
        self.psum = psum
        self.name = name


def _deps(rd, wr, me):
    need = {}

    def add(e, c):
        if e is me and me.in_order:
            return
        if need.get(e, 0) < c:
            need[e] = c

    for b in rd:
        if b.w is not None:
            add(*b.w)
        if b.psum:
            for e, c in b.r.items():
                if e is not me:
                    add(e, c)
    for b in wr:
        if b.w is not None:
            add(*b.w)
        for e, c in b.r.items():
            add(e, c)
    return need


def op(me, fn, rd=(), wr=()):
    need = _deps(rd, wr, me)
    for e, c in need.items():
        if me.waited.get(e, 0) >= c:
            continue
        me.eng.wait_ge(e.sem, c)
        me.waited[e] = c
    ins = fn()
    me.n += 1
    ins.then_inc(me.sem, 1)
    for b in rd:
        b.r[me] = me.n
    for b in wr:
        b.w = (me, me.n)
        b.r = {}
    return ins


def dma(issuer, dsem, out_ap, in_ap, rd=(), wr=()):
    need = _deps(rd, wr, dsem)
    for e, c in need.items():
        if issuer.waited.get(e, 0) >= c:
            continue
        issuer.eng.wait_ge(e.sem, c)
        issuer.waited[e] = c
    ins = issuer.eng.dma_start(out=out_ap, in_=in_ap)
    dsem.n += 16
    ins.then_inc(dsem.sem, 16)
    for b in rd:
        b.r[dsem] = dsem.n
    for b in wr:
        b.w = (dsem, dsem.n)
        b.r = {}
    return ins


def build(ntiles=NT, debug=False, worder=None):
    nc = bass.Bass("TRN2", target_bir_lowering=False)
    dbg_outs = {}

    def din(name, shape, dt=F32):
        return nc.dram_tensor(name, list(shape), dt, kind="ExternalInput").ap()

    xT = din("xT", [D, S])
    yT = nc.dram_tensor("yT", [D, S], F32, kind="ExternalOutput").ap()
    w1g, w1u, w1d = din("w1g", [D, DFF]), din("w1u", [D, DFF]), din("w1d", [DFF, D])
    w2g, w2u, w2d = din("w2g", [D, DFF]), din("w2u", [D, DFF]), din("w2d", [DFF, D])
    win = din("win", [D, 3592])
    wout = din("wout", [D, D])
    params_d = din("params", [128, NPAR])
    cf_d = din("cf", [128, NCF])
    cb_d = din("cb", [128, NCB])
    cs_d = din("cossin", [128, 2, S])
    gm_d = din("gmk", [128, 16, 2, 256])
    koh_d = din("koh", [64, S])
    von_d = din("vones", [128, 8, 128])
    Kd = nc.dram_tensor("Kd", [8, 128, S], BF16).ap()
    Vd = nc.dram_tensor("Vd", [S, 8, 128], BF16).ap()

    es = contextlib.ExitStack()
    with es:
        def sb(name, shape, dt=F32):
            return es.enter_context(nc.sbuf_tensor("sb_" + name, list(shape), dt))

        PE = Eng("pe", nc.tensor, in_order=True)
        ACT = Eng("act", nc.scalar)
        DVE = Eng("dve", nc.vector)
        POOL = Eng("pool", nc.gpsimd)
        SP = Eng("sp", nc.sync)
        dsems = {}

        def dsem(name):
            if name not in dsems:
                e = Eng(name, None)
                e.sem = es.enter_context(nc.semaphore("d_" + name))
                dsems[name] = e
            return dsems[name]

        for e in (PE, ACT, DVE, POOL, SP):
            e.sem = es.enter_context(nc.semaphore("s_" + e.name))

        NB = 8
        NROT = 6
        banks = []
        for k in range(NB):
            t = es.enter_context(nc.psum_tensor(f"pb{k}", [128, 512], F32))
            banks.append((t, Buf(f"pb{k}", psum=True)))
        def mkpool(idxs):
            st = {"k": 0}

            def alloc():
                k = idxs[st["k"] % len(idxs)]
                st["k"] += 1
                return banks[k]
            return alloc

        P_ALL = mkpool(list(range(8)))
        P_FFN = mkpool([0, 1])
        P_GDN = mkpool([2, 3])
        bank = P_ALL

        class _BV:
            def __init__(self, t):
                self.t = t

            def __getitem__(self, key):
                assert key == (slice(None), slice(0, 128))
                return self.t[:, 0:64].bitcast(BF16)

        def bbank(alloc=None):
            t, B = (alloc or bank)()
            return _BV(t), B

        par = sb("par", [128, NPAR]); parB = Buf("par")
        cf = sb("cf", [128, NCF]); cb = sb("cb", [128, NCB], BF16); cB = Buf("consts")
        kmT = sb("kmT", [128, 8, 32], BF16); kmB = Buf("kmT")
        S_f = sb("S_f", [128, 4, 128]); S_b = sb("S_b", [128, 4, 128], BF16)
        SfB = [Buf(f"Sf{h}") for h in range(4)]; SbB = [Buf(f"Sb{h}") for h in range(4)]
        hist = sb("hist", [128, 12, 3]); histB = [Buf(f"hist{c}") for c in range(12)]
        class Ctx:
            pass
        ctxs = []
        for k in range(2):
            cx = Ctx()
            cx.xs = sb(f"xs{k}", [128, 8, T])
            cx.xsB = [Buf(f"xs{k}_{c}") for c in range(8)]
            ctxs.append(cx)
        hTf = sb("hTf", [128, 8, T], BF16); hBf = [Buf(f"hf{c}") for c in range(8)]
        hTm = sb("hTm", [128, 8, T], BF16); hBm = [Buf(f"hm{c}") for c in range(8)]
        bft2 = sb("bft2", [128, T], BF16); bft2B = Buf("bft2")
        actT = sb("actT", [128, 11, T], BF16); actB = [Buf(f"act{c}") for c in range(11)]
        NSLOT = 4
        wslots = [sb(f"wslot{k}", [128, 4096], BF16) for k in range(NSLOT)]
        wslotB = [Buf(f"wslot{k}") for k in range(NSLOT)]
        NTMP = 6
        tmps = [sb(f"tmp{k}", [128, T]) for k in range(NTMP)]
        tmpB = [Buf(f"tmp{k}") for k in range(NTMP)]

        def mktmp(idxs):
            st = {"k": 0}

            def alloc():
                k = idxs[st["k"] % len(idxs)]
                st["k"] += 1
                return tmps[k], tmpB[k]
            return alloc

        tmpF = mktmp([0, 1])
        tmp = mktmp([2, 3, 4, 5])
        tmpA = mktmp([2, 3])
        tmpB_ = mktmp([4, 5])

        pcx = sb("pcx", [128, T + 3]); pcxB = Buf("pcx")
        qT_b = sb("qT_b", [128, 4, T], BF16); qTB = [Buf(f"qT{h}") for h in range(4)]
        kT_b = sb("kT_b", [128, 4, T], BF16); kTB = [Buf(f"kT{h}") for h in range(4)]
        ktok = sb("ktok", [128, 4, 4, 128], BF16); ktokB = [[Buf() for _ in range(4)] for _ in range(4)]
        vtok = sb("vtok", [128, 4, 4, 128], BF16); vtokB = [[Buf() for _ in range(4)] for _ in range(4)]
        ztok = sb("ztok", [128, 4, T], BF16); ztokB = [Buf() for _ in range(4)]
        bft = sb("bft", [128, T], BF16); bftB = Buf("bft")
        qx = sb("qx", [128, 8, T], BF16); mqB = [Buf() for _ in range(4)]
        mkT = sb("mkT", [128, 4, T], BF16); mkB = [Buf() for _ in range(4)]
        vt = sb("vt", [128, 4, 512], BF16); vtB = Buf("vt")
        PTs = [sb(f"PT{k}", [128, T], BF16) for k in range(4)]; PTB = [Buf(f"PT{k}") for k in range(4)]
        rs, rsB = tmps[2], tmpB[2]
        cs = sb("cs", [128, 2, T]); csB = Buf("cs")
        gmt = sb("gmt", [128, 2, 2, 256]); gmtB = Buf("gmt")
        gmm = sb("gmm", [128, 256]); gmmB = Buf("gmm")
        bsel = sb("bsel", [128, 256]); bselB = Buf("bsel")
        top8 = sb("top8", [128, 8, 8]); top8B = Buf("top8")
        km12 = sb("km12", [128, 2]); km12B = Buf("km12")
        gat = sb("gat", [128, 4, 8]); gatB = Buf("gat")
        gg = sb("gg", [128, 4, 4]); ggB = Buf("gg")
        beta = sb("beta", [128, 4, 4]); betaB = Buf("beta")
        gtmp = sb("gtmp", [128, 4, 4]); gtmpB = Buf("gtmp")
        egs = sb("egs", [128, 12]); egsB = Buf("egs")
        bge = sb("bge", [128, 4]); bgeB = Buf("bge")
        ssq = sb("ssq", [128, 4]); ssqB = [Buf() for _ in range(4)]
        rsd = sb("rsd", [128, 4]); rsdB = [Buf() for _ in range(4)]
        mixT = sb("mixT", [128, 8, T], BF16); mixB = [Buf(f"mix{c}") for c in range(8)]
        Kp = [[sb(f"Kp{a}_{k}", [128, 512], BF16) for k in range(2)] for a in range(2)]
        KpB = [[Buf() for k in range(2)] for a in range(2)]
        Vp = [[sb(f"Vp{a}_{k}", [128, 4, 128], BF16) for k in range(2)] for a in range(2)]
        VpB = [[Buf() for k in range(2)] for a in range(2)]
        KdB = [Buf(f"Kd{h}") for h in range(8)]
        VdB = Buf("Vd")
        _gs = sb("Gs", [128, 128]); _gsB = Buf("Gs")
        Gs = [_gs] * 4; GsB = [_gsB] * 4
        Dst = [sb(f"Dst{h}", [128, 128]) for h in range(4)]; DstB = [Buf() for _ in range(4)]
        DTi = [sb(f"DTi{h}", [128, 128]) for h in range(4)]; DTiB = [Buf() for _ in range(4)]
        Ap = [sb(f"Ap{h}", [128, 128]) for h in range(4)]; ApB = [Buf() for _ in range(4)]
        Bp = [sb(f"Bp{h}", [128, 128]) for h in range(4)]; BpB = [Buf() for _ in range(4)]
        inT = [sb(f"inT{h}", [128, 128], BF16) for h in range(4)]; inTB = [Buf() for _ in range(4)]
        Xs = [sb(f"X{h}", [128, 256]) for h in range(4)]; XsB = [Buf() for _ in range(4)]
        kdec = [sb(f"kdec{h}", [128, 128], BF16) for h in range(4)]; kdecB = [Buf() for _ in range(4)]
        wT = [sb(f"wT{h}", [128, 128], BF16) for h in range(4)]; wTB = [Buf() for _ in range(4)]
        vnew = [sb(f"vnew{h}", [128, 128], BF16) for h in range(4)]; vnewB = [Buf() for _ in range(4)]
        ofp = [sb(f"ofp{h}", [128, 128]) for h in range(4)]; ofpB = [Buf() for _ in range(4)]
        otm = [sb(f"otm{h}", [128, 128]) for h in range(4)]; otmB = [Buf() for _ in range(4)]
        ogb = [sb(f"ogb{h}", [128, 128], BF16) for h in range(4)]; ogbB = [Buf() for _ in range(4)]

        ident_bf = cb[:, CB_ID:CB_ID + 128]
        ones_bf = cb[:, CB_ONES:CB_ONES + 128]
        ones_bd = cb[:, CB_ONESBD:CB_ONESBD + 128]
        maskS = cb[:, CB_MS:CB_MS + 128]
        maskIT = cb[:, CB_MIT:CB_MIT + 128]
        ident_f = cf[:, CF_ID:CF_ID + 128]
        Ui_f = cf[:, CF_UI:CF_UI + 128]
        Lst_f = cf[:, CF_LST:CF_LST + 128]
        ones_f = cf[:, CF_ONES:CF_ONES + 128]
        Rm_f = cf[:, CF_RM:CF_RM + 128]

        def cmask(jj):
            return cb[:, CB_CM + jj * 512:CB_CM + (jj + 1) * 512]

        def dump(name, ap, shape, bufs):
            if not debug:
                return
            d = nc.dram_tensor("dbg_" + name, list(shape), ap.dtype, kind="ExternalOutput").ap()
            dbg_outs[name] = d
            dma(SP, dsem("dbg"), d, ap, rd=bufs)

        dma(SP, dsem("c0"), par[:], params_d[:, :], wr=[parB])
        dma(SP, dsem("c1"), cf[:], cf_d[:, :], wr=[cB])
        for c0 in range(0, NCB, 1024):
            c1 = min(NCB, c0 + 1024)
            dma(POOL, dsem("c2"), cb[:, c0:c1], cb_d[:, c0:c1], wr=[cB])
        op(DVE, lambda: nc.vector.memset(hist[:], 0.0), wr=histB)
        op(DVE, lambda: nc.vector.memset(S_f[:], 0.0), wr=SfB)
        op(DVE, lambda: nc.vector.memset(S_b[:], 0.0), wr=SbB)
        op(DVE, lambda: nc.vector.memset(kmT[:], 0.0), wr=[kmB])
        op(DVE, lambda: nc.vector.memset(qx[:], 0.0), wr=mqB)
        op(ACT, lambda: nc.scalar.activation(out=par[:, P_ALOG:P_ALOG + 16], in_=par[:, P_ALOG:P_ALOG + 16], func=AF.Exp), rd=[parB], wr=[parB])
        op(DVE, lambda: nc.vector.tensor_scalar_mul(out=par[:, P_ALOG:P_ALOG + 16], in0=par[:, P_ALOG:P_ALOG + 16], scalar1=-1.0), rd=[parB], wr=[parB])

        WMAP = {"w1g": w1g, "w1u": w1u, "w1d": w1d, "w2g": w2g, "w2u": w2u, "w2d": w2d, "win": win, "wout": wout}

        def ffn_slabs(n):
            wg, wu, wd = f"w{n}g", f"w{n}u", f"w{n}d"
            for fh in range(2):
                f0 = fh * 1408
                for (n0, ncl) in ((0, 512), (512, 512), (1024, 384)):
                    yield (wg, 0, 8, f0 + n0, ncl)
                    yield (wu, 0, 8, f0 + n0, ncl)
                for dg in range(4):
                    yield (wd, f0, 11, dg * 256, 256)

        def inproj_slabs():
            for c0 in (0, 512, 1024, 1536):
                yield ("win", 0, 8, c0, 512)
            yield ("win", 0, 8, 2048, 8)
            for c0 in (2056, 2568, 3080):
                yield ("win", 0, 8, c0, 512)

        def outproj_slabs():
            for c0 in (0, 512):
                yield ("wout", 0, 8, c0, 512)

        record = worder is None
        wrec = []
        wq = []
        wst = {"n": 0, "next": 0}

        def w_load(k, spec):
            (wn, r0, kc, c0, ncl) = spec
            w = WMAP[wn]
            view = wslots[k][:, 0:kc * ncl].rearrange("p (c n) -> p c n", c=kc)
            src = w[r0:r0 + kc * 128, c0:c0 + ncl].rearrange("(c p) n -> p c n", p=128)
            dma(POOL, dsem(f"w{k}"), view, src, wr=[wslotB[k]])
            return (view, wslotB[k], k)

        def w_issue(k):
            if record or wst["next"] >= len(worder):
                return
            spec = worder[wst["next"]]
            wst["next"] += 1
            wq.append((spec, w_load(k, spec)))

        if not record:
            for k in range(NSLOT):
                w_issue(k)

        def init_kv():
            for h in range(8):
                for c0 in (0, 2048):
                    dma(POOL, dsem(f"kinit{h}"), Kd[h, 64:128, c0:c0 + 2048], koh_d[:, c0:c0 + 2048], wr=[KdB[h]])
            for t0 in range(0, S, 128):
                dma(POOL, dsem("vinit"), Vd[t0:t0 + 128, :, :], von_d[:, :, :], wr=[VdB])


        def wget(spec):
            if record:
                wrec.append(spec)
                k = wst["n"] % NSLOT
                wst["n"] += 1
                return w_load(k, spec)
            sp, v = wq.pop(0)
            assert sp == spec, (sp, spec)
            return v

        def wdone(k):
            w_issue(k)

        def mm(out, lhsT, rhs, start, stop, rd, wr, **kw):
            return op(PE, lambda: nc.tensor.matmul(out, lhsT=lhsT, rhs=rhs, start=start, stop=stop, **kw), rd=rd, wr=wr)

        def act(out, in_, func, rd, wr, **kw):
            return op(ACT, lambda: nc.scalar.activation(out=out, in_=in_, func=func, **kw), rd=rd, wr=wr)

        def sigmoid_inplace(buf_ap, in_ap, rd, B):
            act(buf_ap, in_ap, AF.Exp, rd=rd, wr=[B], scale=-1.0)
            act(buf_ap, buf_ap, AF.Ln, rd=[B], wr=[B], bias=1.0)
            act(buf_ap, buf_ap, AF.Exp, rd=[B], wr=[B], scale=-1.0)

        def rstd_from(out_ap, in_ap, scale, rd, B):
            act(out_ap, in_ap, AF.Ln, rd=rd, wr=[B], scale=scale, bias=EPS)
            act(out_ap, out_ap, AF.Exp, rd=[B], wr=[B], scale=-0.5)

        def barrier():
            engs = (PE, ACT, DVE, POOL)
            for a in engs:
                for b in engs:
                    if a is b or b.n == 0:
                        continue
                    if a.waited.get(b, 0) >= b.n:
                        continue
                    a.eng.wait_ge(b.sem, b.n)
                    a.waited[b] = b.n

        def norm_h(gcol, X, bank, mixer, talloc=None):
            xs, xsB = X.xs, X.xsB
            hT, hB = (hTm, hBm) if mixer else (hTf, hBf)
            talloc = talloc or (tmp if mixer else tmpF)
            pb, pB = bank()
            if mixer:
                for c in range(8):
                    act(bft2[:, :], xs[:, c, :], AF.Square, rd=[xsB[c]], wr=[bft2B])
                    mm(pb[:, :], ones_bf, bft2[:, :], c == 0, c == 7, rd=[bft2B, cB], wr=[pB])
                    if c % 2 == 1:
                        yield
            else:
                for c in range(8):
                    act(actT[:, c, :], xs[:, c, :], AF.Square, rd=[xsB[c]], wr=[actB[c]])
                yield
                for c in range(8):
                    mm(pb[:, :], ones_bf, actT[:, c, :], c == 0, c == 7, rd=[actB[c], cB], wr=[pB])
            r, rB = talloc()
            rstd_from(r[:, :], pb[:, :], 1.0 / D, [pB], rB)
            yield
            for c in range(8):
                op(DVE, lambda: nc.vector.scalar_tensor_tensor(out=hT[:, c, :], in0=xs[:, c, :], scalar=par[:, gcol + c:gcol + c + 1], in1=r[:, :], op0=ALU.mult, op1=ALU.mult),
                   rd=[xsB[c], rB, parB], wr=[hB[c]])
                if c % 4 == 3:
                    yield

        def ffn(n, gcol, X, bank=P_ALL):
            xs, xsB = X.xs, X.xsB
            hT, hB = hTf, hBf
            sl_ = ffn_slabs(n)
            yield from norm_h(gcol, X, bank, False)
            for fh in range(2):
                for (n0, ncl) in ((0, 512), (512, 512), (1024, 384)):
                    gv, gB, gk = wget(next(sl_))
                    uv, uB, uk = wget(next(sl_))
                    for j in range(ncl // 128):
                        fl = n0 // 128 + j
                        pg, pgB = bank()
                        pu, puB = bank()
                        for c in range(8):
                            mm(pg[:, :], gv[:, c, j * 128:(j + 1) * 128], hT[:, c, :], c == 0, c == 7, rd=[gB, hB[c]], wr=[pgB])
                        for c in range(8):
                            mm(pu[:, :], uv[:, c, j * 128:(j + 1) * 128], hT[:, c, :], c == 0, c == 7, rd=[uB, hB[c]], wr=[puB])
                        e, eB = tmpF()
                        sigmoid_inplace(e[:, :], pg[:, :], [pgB], eB)
                        t, tB = tmpF()
                        op(DVE, lambda: nc.vector.tensor_tensor(out=t[:, :], in0=pg[:, :], in1=e[:, :], op=ALU.mult), rd=[pgB, eB], wr=[tB])
                        op(DVE, lambda: nc.vector.tensor_tensor(out=actT[:, fl, :], in0=pu[:, :], in1=t[:, :], op=ALU.mult), rd=[puB, tB], wr=[actB[fl]])
                        yield
                    wdone(gk)
                    wdone(uk)
                for dg in range(4):
                    wv, wB, wk = wget(next(sl_))
                    for dd in range(2):
                        d = dg * 2 + dd
                        po, poB = bank()
                        for f in range(11):
                            mm(po[:, :], wv[:, f, dd * 128:(dd + 1) * 128], actT[:, f, :], f == 0, f == 10, rd=[wB, actB[f]], wr=[poB])
                        op(DVE, lambda: nc.vector.scalar_tensor_tensor(out=xs[:, d, :], in0=po[:, :], scalar=0.5, in1=xs[:, d, :], op0=ALU.mult, op1=ALU.add),
                           rd=[poB, xsB[d]], wr=[xsB[d]])
                        yield
                    wdone(wk)

        def gdn_qkv(bank, sl_):
            hT, hB = hTm, hBm
            for which in range(3):
                wv, wB, wk = wget(next(sl_))
                for hc in range(4):
                    c12 = which * 4 + hc
                    pb, pB = bank()
                    for c in range(8):
                        mm(pb[:, :], wv[:, c, hc * 128:(hc + 1) * 128], hT[:, c, :], c == 0, c == 7, rd=[wB, hB[c]], wr=[pB])
                    if hc == 3:
                        wdone(wk)
                    op(DVE, lambda: nc.vector.tensor_copy(pcx[:, 0:3], hist[:, c12, :]), rd=[histB[c12]], wr=[pcxB])
                    act(pcx[:, 3:T + 3], pb[:, :], AF.Copy, rd=[pB], wr=[pcxB])
                    op(DVE, lambda: nc.vector.tensor_copy(hist[:, c12, :], pcx[:, T:T + 3]), rd=[pcxB], wr=[histB[c12]])
                    ca, caB = tmpA()
                    cw = P_CONV + c12 * 4
                    op(DVE, lambda: nc.vector.tensor_scalar_mul(out=ca[:, :], in0=pcx[:, 0:T], scalar1=par[:, cw:cw + 1]), rd=[pcxB, parB], wr=[caB])
                    for k in range(1, 4):
                        op(DVE, lambda: nc.vector.scalar_tensor_tensor(out=ca[:, :], in0=pcx[:, k:k + T], scalar=par[:, cw + k:cw + k + 1], in1=ca[:, :], op0=ALU.mult, op1=ALU.add),
                           rd=[pcxB, parB, caB], wr=[caB])
                    e, eB = tmpA()
                    sigmoid_inplace(e[:, :], ca[:, :], [caB], eB)
                    if which == 2:
                        op(DVE, lambda: nc.vector.tensor_tensor(out=bft[:, :], in0=ca[:, :], in1=e[:, :], op=ALU.mult), rd=[caB, eB], wr=[bftB])
                        yield
                        for blk in range(4):
                            pt, ptB = bbank(bank)
                            op(PE, lambda: nc.tensor.transpose(out=pt[:, 0:128], in_=bft[:, blk * 128:(blk + 1) * 128], identity=ident_bf), rd=[bftB, cB], wr=[ptB])
                            act(vtok[:, blk, hc, :], pt[:, 0:128], AF.Copy, rd=[ptB], wr=[vtokB[blk][hc]])
                        yield
                        continue
                    op(DVE, lambda: nc.vector.tensor_tensor(out=ca[:, :], in0=ca[:, :], in1=e[:, :], op=ALU.mult), rd=[caB, eB], wr=[caB])
                    act(bft[:, :], ca[:, :], AF.Square, rd=[caB], wr=[bftB])
                    yield
                    p2, p2B = bank()
                    mm(p2[:, :], ones_bf, bft[:, :], True, True, rd=[bftB, cB], wr=[p2B])
                    rstd_from(e[:, :], p2[:, :], 1.0, [p2B], eB)
                    dst, dB = (qT_b, qTB) if which == 0 else (kT_b, kTB)
                    sc = (128.0 ** -0.5) if which == 0 else 1.0
                    op(DVE, lambda: nc.vector.scalar_tensor_tensor(out=dst[:, hc, :], in0=ca[:, :], scalar=sc, in1=e[:, :], op0=ALU.mult, op1=ALU.mult), rd=[caB, eB], wr=[dB[hc]])
                    yield
                    if which == 1:
                        for blk in range(4):
                            pt, ptB = bbank(bank)
                            op(PE, lambda: nc.tensor.transpose(out=pt[:, 0:128], in_=kT_b[:, hc, blk * 128:(blk + 1) * 128], identity=ident_bf), rd=[kTB[hc], cB], wr=[ptB])
                            act(ktok[:, blk, hc, :], pt[:, 0:128], AF.Copy, rd=[ptB], wr=[ktokB[blk][hc]])
                        yield

        def gdn_z_gates(bank, sl_):
            hT, hB = hTm, hBm
            wv, wB, wk = wget(next(sl_))
            for blk in range(4):
                pb, pB = bank()
                for c in range(8):
                    mm(pb[:, :], hT[:, c, blk * 128:(blk + 1) * 128], wv[:, c, :], c == 0, c == 7, rd=[wB, hB[c]], wr=[pB])
                e, eB = tmpB_()
                sigmoid_inplace(e[:, :], pb[:, :], [pB], eB)
                op(DVE, lambda: nc.vector.tensor_tensor(out=ztok[:, blk, :], in0=pb[:, :], in1=e[:, :], op=ALU.mult), rd=[pB, eB], wr=[ztokB[blk]])
                yield
            wdone(wk)
            wv, wB, wk = wget(next(sl_))
            pb, pB = bank()
            for blk in range(4):
                for c in range(8):
                    mm(pb[:, blk * 8:(blk + 1) * 8], hT[:, c, blk * 128:(blk + 1) * 128], wv[:, c, :], c == 0, c == 7, rd=[wB, hB[c]], wr=[pB])
            wdone(wk)
            act(gat[:, :, :], pb[:, 0:32].rearrange("p (b k) -> p b k", b=4), AF.Copy, rd=[pB], wr=[gatB])
            op(DVE, lambda: nc.vector.tensor_tensor(out=gtmp[:, :, :], in0=gat[:, :, 0:4], in1=par[:, P_DTB:P_DTB + 16].rearrange("p (b k) -> p b k", b=4), op=ALU.add), rd=[gatB, parB], wr=[gtmpB])
            act(gtmp[:, :, :], gtmp[:, :, :], AF.Exp, rd=[gtmpB], wr=[gtmpB])
            act(gtmp[:, :, :], gtmp[:, :, :], AF.Ln, rd=[gtmpB], wr=[gtmpB], bias=1.0)
            op(DVE, lambda: nc.vector.tensor_tensor(out=gg[:, :, :], in0=gtmp[:, :, :], in1=par[:, P_ALOG:P_ALOG + 16].rearrange("p (b k) -> p b k", b=4), op=ALU.mult), rd=[gtmpB, parB], wr=[ggB])
            sigmoid_inplace(beta[:, :, :], gat[:, :, 4:8], [gatB], betaB)
            yield

        def moba_qkv(i, bank, sl_):
            hT, hB = hTm, hBm
            dma(SP, dsem("cs"), cs[:], cs_d[:, :, i * T:(i + 1) * T], wr=[csB])
            dma(SP, dsem("gm"), gmt[:], gm_d[:, 2 * i:2 * i + 2, :, :], wr=[gmtB])
            for which in range(2):
                wv, wB, wk = wget(next(sl_))
                for c in range(4):
                    pb, pB = bank()
                    for cc in range(8):
                        mm(pb[:, :], wv[:, cc, c * 128:(c + 1) * 128], hT[:, cc, :], cc == 0, cc == 7, rd=[wB, hB[cc]], wr=[pB])
                    if c == 3:
                        wdone(wk)
                    xf, xfB = tmpB_()
                    act(xf[:, :], pb[:, :], AF.Copy, rd=[pB], wr=[xfB])
                    act(bft2[:, :], pb[:, :], AF.Square, rd=[pB], wr=[bft2B])
                    yield
                    p2, p2B = bank()
                    mm(p2[:, :], ones_bd, bft2[:, :], True, True, rd=[bft2B, cB], wr=[p2B])
                    r, rB = tmpB_()
                    rstd_from(r[:, :], p2[:, :], 1.0 / 64.0, [p2B], rB)
                    gc = P_QG if which == 0 else P_KG
                    op(DVE, lambda: nc.vector.scalar_tensor_tensor(out=xf[:, :], in0=xf[:, :], scalar=par[:, gc:gc + 1], in1=r[:, :], op0=ALU.mult, op1=ALU.mult), rd=[xfB, rB, parB], wr=[xfB])
                    yield
                    p3, p3B = bank()
                    mm(p3[:, :], Rm_f, xf[:, :], True, True, rd=[xfB, cB], wr=[p3B])
                    op(DVE, lambda: nc.vector.tensor_tensor(out=r[:, :], in0=p3[:, :], in1=cs[:, 1, :], op=ALU.mult), rd=[p3B, csB], wr=[rB])
                    op(DVE, lambda: nc.vector.tensor_tensor(out=xf[:, :], in0=xf[:, :], in1=cs[:, 0, :], op=ALU.mult), rd=[xfB, csB], wr=[xfB])
                    op(DVE, lambda: nc.vector.tensor_tensor(out=xf[:, :], in0=xf[:, :], in1=r[:, :], op=ALU.add), rd=[xfB, rB], wr=[xfB])
                    if which == 0:
                        act(qx[0:64, 2 * c, :], xf[0:64, :], AF.Copy, rd=[xfB], wr=[mqB[c]])
                        act(qx[0:64, 2 * c + 1, :], xf[64:128, :], AF.Copy, rd=[xfB, mqB[c]], wr=[mqB[c]])
                    else:
                        act(mkT[:, c, :], xf[:, :], AF.Copy, rd=[xfB], wr=[mkB[c]])
                        for hp in range(2):
                            dma(SP, dsem(f"kst{2 * c + hp}"), Kd[2 * c + hp, 0:64, i * T:(i + 1) * T], mkT[hp * 64:(hp + 1) * 64, c, :], rd=[mkB[c]], wr=[KdB[2 * c + hp]])
                        op(DVE, lambda: nc.vector.tensor_reduce(out=km12[:, 0:1], in_=xf[:, 0:256], axis=AX.X, op=ALU.add), rd=[xfB], wr=[km12B])
                        op(DVE, lambda: nc.vector.tensor_reduce(out=km12[:, 1:2], in_=xf[:, 256:512], axis=AX.X, op=ALU.add), rd=[xfB, km12B], wr=[km12B])
                        act(kmT[0:64, 2 * c, 2 * i:2 * i + 2], km12[0:64, 0:2], AF.Copy, rd=[km12B], wr=[kmB], scale=1.0 / 256.0)
                        act(kmT[0:64, 2 * c + 1, 2 * i:2 * i + 2], km12[64:128, 0:2], AF.Copy, rd=[km12B, kmB], wr=[kmB], scale=1.0 / 256.0)
                    yield
            wv, wB, wk = wget(next(sl_))
            for sub in range(4):
                pb, pB = bank()
                for cc in range(8):
                    mm(pb[:, :], hT[:, cc, sub * 128:(sub + 1) * 128], wv[:, cc, :], cc == 0, cc == 7, rd=[wB, hB[cc]], wr=[pB])
                act(vt[:, sub, :], pb[:, :], AF.Copy, rd=[pB], wr=[vtB])
                yield
            wdone(wk)
            for h in range(8):
                off = 0 if h % 2 == 0 else 64
                dma(SP, dsem("vst"), Vd[i * T:(i + 1) * T, h, off:off + 64].rearrange("(s p) f -> p s f", p=128), vt[:, :, h * 64:(h + 1) * 64], rd=[vtB], wr=[VdB])

        def moba_gate(i, bank):
            for sub in range(4):
                ow = sub // 2
                pb, pB = bank()
                for h in range(8):
                    c, hp = h // 2, h % 2
                    mm(pb[:, h * 32:(h + 1) * 32], qx[:, h, sub * 128:(sub + 1) * 128], kmT[:, h, :], True, True,
                       rd=[mqB[c], kmB], wr=[pB])
                op(DVE, lambda: nc.vector.tensor_tensor(out=gmm[:, :], in0=pb[:, 0:256], in1=gmt[:, ow, 0, :], op=ALU.add), rd=[pB, gmtB], wr=[gmmB])
                for h in range(8):
                    op(DVE, lambda: nc.vector.max(out=top8[:, h, :], in_=gmm[:, h * 32:(h + 1) * 32]), rd=[gmmB], wr=[top8B])
                for h in range(8):
                    op(DVE, lambda: nc.vector.tensor_scalar(out=bsel[:, h * 32:(h + 1) * 32], in0=gmm[:, h * 32:(h + 1) * 32], scalar1=top8[:, h, 2:3], scalar2=1.0, op0=ALU.is_ge, op1=ALU.subtract),
                       rd=[gmmB, top8B], wr=[bselB])
                op(DVE, lambda: nc.vector.scalar_tensor_tensor(out=bsel[:, :], in0=bsel[:, :], scalar=-NEG, in1=gmt[:, ow, 1, :], op0=ALU.mult, op1=ALU.mult), rd=[bselB, gmtB], wr=[bselB])
                yield
                for g2 in range(2):
                    pt, ptB = bank()
                    op(PE, lambda: nc.tensor.transpose(out=pt[:, 0:128], in_=bsel[:, g2 * 128:(g2 + 1) * 128], identity=ident_f), rd=[bselB, cB], wr=[ptB])
                    for g in range(4):
                        hh = g2 * 4 + g
                        act(qx[64:80, hh, sub * 128:(sub + 1) * 128], pt[g * 32:g * 32 + 16, 0:128], AF.Copy, rd=[ptB, mqB[hh // 2]], wr=[mqB[hh // 2]])
                yield

        def moba_stream(i, sid, heads):
            nj = 4 * i + 4
            npc = (nj + 3) // 4
            ps_, psB = banks[4 + sid]
            po, poB = banks[6 + sid]
            ring = {"k": 0}
            pieces = [(h, pc) for h in heads for pc in range(npc)]

            def load(idx):
                h, pc = pieces[idx]
                k = idx % 2
                nk = min(4, nj - pc * 4)
                dma(SP, dsem(f"kl{sid}{k}"), Kp[sid][k][:, 0:nk * 128], Kd[h, :, pc * 512:pc * 512 + nk * 128], rd=[KdB[h]], wr=[KpB[sid][k]])
                dma(SP, dsem(f"vl{sid}{k}"), Vp[sid][k][:, 0:nk, :], Vd[pc * 512:pc * 512 + nk * 128, h, :].rearrange("(j p) f -> p j f", p=128), rd=[VdB], wr=[VpB[sid][k]])

            load(0)
            pend = None
            ptk = 0
            for idx, (h, pc) in enumerate(pieces):
                if pend is not None:
                    pend()
                    pend = None
                if idx + 1 < len(pieces):
                    load(idx + 1)
                k = idx % 2
                c, hp = h // 2, h % 2
                nk = min(4, nj - pc * 4)
                for jj in range(nk):
                    j = pc * 4 + jj
                    diag = j >= 4 * i
                    mm(ps_[:, :], Kp[sid][k][:, jj * 128:(jj + 1) * 128], qx[:, h, :], True, not diag, rd=[KpB[sid][k], mqB[c]], wr=[psB])
                    if diag:
                        mm(ps_[:, :], ident_bf, cmask(j - 4 * i), False, True, rd=[cB], wr=[psB])
                    if pend is not None:
                        pend()
                        pend = None
                    PT, PB = PTs[sid * 2 + ptk % 2], PTB[sid * 2 + ptk % 2]
                    ptk += 1
                    act(PT[:, :], ps_[:, :], AF.Exp, rd=[psB], wr=[PB], scale=0.125)

                    def pv(PT=PT, PB=PB, j=j, jj=jj, k=k):
                        mm(po[:, :], Vp[sid][k][:, jj, :], PT[:, :], j == 0, j == nj - 1, rd=[VpB[sid][k], PB], wr=[poB])
                    pend = pv
                    if j == nj - 1:
                        pend()
                        pend = None
                        osl = slice(hp * 64, (hp + 1) * 64)
                        ssl = slice((1 - hp) * 64, (2 - hp) * 64)
                        act(rs[osl, :], po[ssl, :], AF.Ln, rd=[poB], wr=[rsB])
                        act(rs[osl, :], rs[osl, :], AF.Exp, rd=[rsB], wr=[rsB], scale=-1.0)
                        op(DVE, lambda: nc.vector.tensor_tensor(out=mixT[osl, 4 + c, :], in0=po[osl, :], in1=rs[osl, :], op=ALU.mult), rd=[poB, rsB, mixB[4 + c]], wr=[mixB[4 + c]])
                    yield

        def gdn_block(blk, gb=P_GDN):
            bs = slice(blk * 128, (blk + 1) * 128)
            pb, pB = gb()
            mm(pb[:, 0:4], Ui_f, gg[:, blk, :], True, True, rd=[ggB, cB], wr=[pB])
            mm(pb[:, 4:8], Lst_f, gg[:, blk, :], True, True, rd=[ggB, cB], wr=[pB])
            mm(pb[:, 8:12], ones_f, gg[:, blk, :], True, True, rd=[ggB, cB], wr=[pB])
            act(egs[:, :], pb[:, 0:12], AF.Exp, rd=[pB], wr=[egsB])
            op(DVE, lambda: nc.vector.tensor_tensor(out=bge[:, :], in0=beta[:, blk, :], in1=egs[:, 0:4], op=ALU.mult), rd=[betaB, egsB], wr=[bgeB])
            yield

            def sD(h):
                op(DVE, lambda: nc.vector.tensor_scalar_mul(out=Gs[h][:, :], in0=Lst_f, scalar1=gg[:, blk, h:h + 1]), rd=[cB, ggB], wr=[GsB[h]])
                p1, p1B = gb()
                mm(p1[:, 0:128], Ui_f, Gs[h][:, :], True, False, rd=[cB, GsB[h]], wr=[p1B])
                mm(p1[:, 0:128], ident_bf, maskS, False, True, rd=[cB], wr=[p1B])
                mm(p1[:, 128:256], Gs[h][:, :], Ui_f, True, False, rd=[cB, GsB[h]], wr=[p1B])
                mm(p1[:, 128:256], ident_bf, maskIT, False, True, rd=[cB], wr=[p1B])
                act(Dst[h][:, :], p1[:, 0:128], AF.Exp, rd=[p1B], wr=[DstB[h]])
                act(DTi[h][:, :], p1[:, 128:256], AF.Exp, rd=[p1B], wr=[DTiB[h]])

            def sG(h):
                p2, p2B = gb()
                mm(p2[:, 0:128], kT_b[:, h, bs], kT_b[:, h, bs], True, True, rd=[kTB[h]], wr=[p2B])
                mm(p2[:, 128:256], kT_b[:, h, bs], qT_b[:, h, bs], True, True, rd=[kTB[h], qTB[h]], wr=[p2B])
                op(DVE, lambda: nc.vector.scalar_tensor_tensor(out=Ap[h][:, :], in0=p2[:, 0:128], scalar=beta[:, blk, h:h + 1], in1=Dst[h][:, :], op0=ALU.mult, op1=ALU.mult),
                   rd=[p2B, betaB, DstB[h]], wr=[ApB[h]])
                op(DVE, lambda: nc.vector.tensor_tensor(out=inT[h][:, :], in0=p2[:, 128:256], in1=DTi[h][:, :], op=ALU.mult), rd=[p2B, DTiB[h]], wr=[inTB[h]])
                op(DVE, lambda: nc.vector.tensor_scalar_mul(out=Xs[h][:, 0:128], in0=vtok[:, blk, h, :], scalar1=beta[:, blk, h:h + 1]), rd=[vtokB[blk][h], betaB], wr=[XsB[h]])
                op(DVE, lambda: nc.vector.tensor_scalar_mul(out=Xs[h][:, 128:256], in0=ktok[:, blk, h, :], scalar1=bge[:, h:h + 1]), rd=[ktokB[blk][h], bgeB, XsB[h]], wr=[XsB[h]])
                op(DVE, lambda: nc.vector.tensor_scalar_mul(out=kdec[h][:, :], in0=ktok[:, blk, h, :], scalar1=egs[:, 4 + h:5 + h]), rd=[ktokB[blk][h], egsB], wr=[kdecB[h]])

            def sT(h):
                pt, ptB = gb()
                op(PE, lambda: nc.tensor.transpose(out=pt[:, 0:128], in_=Ap[h][:, :], identity=ident_f), rd=[ApB[h], cB], wr=[ptB])
                act(Bp[h][:, :], pt[:, 0:128], AF.Copy, rd=[ptB], wr=[BpB[h]])

            def mk_solve(s):
                def f(h):
                    pk, pkB = gb()
                    mm(pk[:, 0:256], Bp[h][:, :], Xs[h][:, :], True, True, rd=[BpB[h], XsB[h]], wr=[pkB])
                    if s < 6:
                        mm(pk[:, 256:384], Bp[h][:, :], Ap[h][:, :], True, True, rd=[BpB[h], ApB[h]], wr=[pkB])
                        mm(pk[:, 384:512], Ap[h][:, :], Bp[h][:, :], True, True, rd=[BpB[h], ApB[h]], wr=[pkB])
                    if s == 0:
                        op(DVE, lambda: nc.vector.scalar_tensor_tensor(out=Xs[h][:, :], in0=pk[:, 0:256], scalar=-1.0, in1=Xs[h][:, :], op0=ALU.mult, op1=ALU.add),
                           rd=[pkB, XsB[h]], wr=[XsB[h]])
                    else:
                        op(DVE, lambda: nc.vector.tensor_tensor(out=Xs[h][:, :], in0=pk[:, 0:256], in1=Xs[h][:, :], op=ALU.add),
                           rd=[pkB, XsB[h]], wr=[XsB[h]])
                    if s < 6:
                        act(Ap[h][:, :], pk[:, 256:384], AF.Copy, rd=[pkB], wr=[ApB[h]])
                        act(Bp[h][:, :], pk[:, 384:512], AF.Copy, rd=[pkB], wr=[BpB[h]])
                return f

            def sW(h):
                pt, ptB = gb()
                op(PE, lambda: nc.tensor.transpose(out=pt[:, 0:128], in_=Xs[h][:, 128:256], identity=ident_f), rd=[XsB[h], cB], wr=[ptB])
                act(wT[h][:, :], pt[:, 0:128], AF.Copy, rd=[ptB], wr=[wTB[h]])

            def sV(h):
                pv, pvB = gb()
                mm(pv[:, 0:128], wT[h][:, :], S_b[:, h, :], True, True, rd=[wTB[h], SbB[h]], wr=[pvB])
                op(DVE, lambda: nc.vector.scalar_tensor_tensor(out=vnew[h][:, :], in0=pv[:, 0:128], scalar=-1.0, in1=Xs[h][:, 0:128], op0=ALU.mult, op1=ALU.add), rd=[pvB, XsB[h]], wr=[vnewB[h]])

            def sO(h):
                po, poB = gb()
                mm(po[:, 0:128], qT_b[:, h, bs], S_b[:, h, :], True, True, rd=[qTB[h], SbB[h]], wr=[poB])
                mm(po[:, 128:256], inT[h][:, :], vnew[h][:, :], True, True, rd=[inTB[h], vnewB[h]], wr=[poB])
                mm(po[:, 256:384], kdec[h][:, :], vnew[h][:, :], True, True, rd=[kdecB[h], vnewB[h]], wr=[poB])
                act(otm[h][:, :], po[:, 0:128], AF.Copy, rd=[poB, egsB], wr=[otmB[h]], scale=egs[:, h:h + 1])
                op(DVE, lambda: nc.vector.scalar_tensor_tensor(out=S_f[:, h, :], in0=S_f[:, h, :], scalar=egs[:, 8 + h:9 + h], in1=po[:, 256:384], op0=ALU.mult, op1=ALU.add),
                   rd=[SfB[h], egsB, poB], wr=[SfB[h]])
                op(DVE, lambda: nc.vector.tensor_tensor(out=ofp[h][:, :], in0=po[:, 128:256], in1=otm[h][:, :], op=ALU.add), rd=[poB, otmB[h]], wr=[ofpB[h]])
                act(S_b[:, h, :], S_f[:, h, :], AF.Copy, rd=[SfB[h]], wr=[SbB[h]])
                op(DVE, lambda: nc.vector.memset(ssq[:, h:h + 1], 0.0), wr=[ssqB[h]])
                act(otm[h][:, :], ofp[h][:, :], AF.Square, rd=[ofpB[h], ssqB[h]], wr=[otmB[h], ssqB[h]], accum_out=ssq[:, h:h + 1])
                rstd_from(rsd[:, h:h + 1], ssq[:, h:h + 1], 1.0 / 128.0, [ssqB[h]], rsdB[h])
                op(DVE, lambda: nc.vector.scalar_tensor_tensor(out=ofp[h][:, :], in0=ofp[h][:, :], scalar=rsd[:, h:h + 1], in1=par[:, P_ONORM:P_ONORM + 128], op0=ALU.mult, op1=ALU.mult),
                   rd=[ofpB[h], rsdB[h], parB], wr=[ofpB[h]])
                op(DVE, lambda: nc.vector.tensor_tensor(out=ogb[h][:, :], in0=ofp[h][:, :], in1=ztok[:, blk, h * 128:(h + 1) * 128], op=ALU.mult), rd=[ofpB[h], ztokB[blk]], wr=[ogbB[h]])

            def sX(h):
                pt, ptB = bbank(gb)
                op(PE, lambda: nc.tensor.transpose(out=pt[:, 0:128], in_=ogb[h][:, :], identity=ident_bf), rd=[ogbB[h], cB], wr=[ptB])
                act(mixT[:, h, bs], pt[:, 0:128], AF.Copy, rd=[ptB], wr=[mixB[h]])

            steps = [sD, sG, sT] + [mk_solve(s) for s in range(7)] + [sW, sV, sO, sX]
            for st in steps:
                for pair in ((0, 1), (2, 3)):
                    for h in pair:
                        st(h)
                    yield

        def out_proj(X, bank=P_ALL):
            xs, xsB = X.xs, X.xsB
            sl_ = outproj_slabs()
            for dgp in range(2):
                wv, wB, wk = wget(next(sl_))
                for dd in range(4):
                    d = dgp * 4 + dd
                    pb, pB = bank()
                    for m in range(8):
                        mm(pb[:, :], wv[:, m, dd * 128:(dd + 1) * 128], mixT[:, m, :], m == 0, m == 7, rd=[wB, mixB[m]], wr=[pB])
                    op(DVE, lambda: nc.vector.tensor_tensor(out=xs[:, d, :], in0=pb[:, :], in1=xs[:, d, :], op=ALU.add), rd=[pB, xsB[d]], wr=[xsB[d]])
                    yield
                wdone(wk)

        def run(*gens):
            gens = [g for g in gens if g is not None]
            dead = set()
            while len(dead) < len(set(map(id, gens))):
                for g in gens:
                    if id(g) in dead:
                        continue
                    try:
                        next(g)
                    except StopIteration:
                        dead.add(id(g))

        def seq(*fns):
            for f in fns:
                r = f()
                if r is not None:
                    yield from r

        def load_x(i, X):
            dma(SP, dsem(f"xl{i % 2}"), X.xs[:, :, :], xT[:, i * T:(i + 1) * T].rearrange("(c p) t -> p c t", p=128), wr=X.xsB)

        def gdn_all():
            for blk in range(4):
                yield from gdn_block(blk)

        P_F3 = mkpool([0, 1, 2])
        P_I5 = mkpool([3, 4, 5, 6, 7])

        def rr(*gens):
            gens = list(gens)
            while gens:
                for g in list(gens):
                    try:
                        next(g)
                        yield
                    except StopIteration:
                        gens.remove(g)

        def inproj(i, X, bank, do_norm=True):
            sl_ = list(inproj_slabs())
            s1 = iter(sl_[0:3])
            s2 = iter(sl_[3:8])
            if do_norm:
                yield from norm_h(P_MIX, X, bank, True)
            yield from rr(gdn_qkv(bank, s1), seq2(lambda: gdn_z_gates(bank, s2), lambda: moba_qkv(i, bank, s2), lambda: moba_gate(i, bank)))

        def seq2(*fns):
            for f in fns:
                yield from f()

        load_x(0, ctxs[0])
        run(ffn(1, P_N1, ctxs[0]))
        init_kv()
        if debug:
            dump("x1", ctxs[0].xs[:, :, :], [128, 8, T], ctxs[0].xsB)
        run(inproj(0, ctxs[0], P_ALL))
        for i in range(ntiles):
            X = ctxs[i % 2]
            Xn = ctxs[(i + 1) % 2]
            last = i + 1 >= ntiles
            if not last:
                load_x(i + 1, Xn)
            g_gdn = gdn_all()
            run(g_gdn, moba_stream(i, 0, [0, 2, 4, 6]), moba_stream(i, 1, [1, 3, 5, 7]), g_gdn, None if last else seq2(lambda: ffn(1, P_N1, Xn, P_FFN), lambda: norm_h(P_MIX, Xn, P_FFN, True, tmpF)))
            if debug and i == 0:
                dump("mixT", mixT[:, :, :], [128, 8, T], mixB)
            if last:
                run(out_proj(X))
                run(ffn(2, P_N2, X, P_ALL))
            else:
                g_in = inproj(i + 1, Xn, P_I5, do_norm=False)
                run(seq2(lambda: out_proj(X, P_F3), lambda: ffn(2, P_N2, X, P_F3)), g_in, g_in)
            dma(SP, dsem("st"), yT[:, i * T:(i + 1) * T].rearrange("(c p) t -> p c t", p=128), X.xs[:, :, :], rd=X.xsB)
        for nm, e in dsems.items():
            if nm.startswith("w") and e.n > 0:
                POOL.eng.wait_ge(e.sem, e.n)
        SP.eng.wait_ge(dsems["st"].sem, dsems["st"].n)
        if debug and "dbg" in dsems:
            SP.eng.wait_ge(dsems["dbg"].sem, dsems["dbg"].n)
    return nc, dbg_outs, wrec


def _consts():
    cbv = np.zeros((128, NCB), np.float32)
    cfv = np.zeros((128, NCF), np.float32)
    I = np.eye(128, dtype=np.float32)
    p = np.arange(128)
    cbv[:, CB_ID:CB_ID + 128] = I
    cbv[:, CB_ONES:CB_ONES + 128] = 1.0
    cbv[:, CB_ONESBD:CB_ONESBD + 128] = (p[:, None] // 64 == p[None, :] // 64)
    cbv[:, CB_MS:CB_MS + 128] = np.where(p[:, None] > p[None, :], 0.0, NEG)
    cbv[:, CB_MIT:CB_MIT + 128] = np.where(p[None, :] >= p[:, None], 0.0, NEG)
    q = np.arange(512)
    cm = np.zeros((128, 4, 512), np.float32)
    for jj in range(4):
        cm[:, jj, :] = np.where((jj * 128 + p)[:, None] <= q[None, :], 0.0, NEG)
    cbv[:, CB_CM:CB_CM + 2048] = cm.reshape(128, 2048)
    cfv[:, CF_ID:CF_ID + 128] = I
    cfv[:, CF_UI:CF_UI + 128] = (p[:, None] <= p[None, :])
    cfv[:, CF_LST:CF_LST + 128] = (p[:, None] > p[None, :])
    cfv[:, CF_ONES:CF_ONES + 128] = 1.0
    rm = np.zeros((128, 128), np.float32)
    for pp in range(128):
        d = pp % 64
        if d < 8:
            rm[pp + 8, pp] = -1.0
        elif d < 16:
            rm[pp - 8, pp] = 1.0
    cfv[:, CF_RM:CF_RM + 128] = rm
    half = 8
    inv_freq = np.power(np.float32(500000.0), -np.arange(half, dtype=np.float32) * 2.0 / 16.0).astype(np.float32)
    pos = np.arange(S, dtype=np.float32)
    ang = pos[:, None] * inv_freq[None, :]
    cosv, sinv = np.cos(ang).astype(np.float32), np.sin(ang).astype(np.float32)
    cs = np.zeros((128, 2, S), np.float32)
    cs[:, 0, :] = 1.0
    for pp in range(128):
        d = pp % 64
        if d < 16:
            cs[pp, 0, :] = cosv[:, d % 8]
            cs[pp, 1, :] = sinv[:, d % 8]
    gm = np.zeros((128, 16, 2, 256), np.float32)
    n = np.arange(32)
    for own in range(16):
        row = np.where(n < min(own, 16), 0.0, NEG).astype(np.float32)
        row[16:] = NEG
        pst = (n < own).astype(np.float32)
        pst[16:] = 0.0
        gm[:, own, 0, :] = np.tile(row, 8)[None, :]
        gm[:, own, 1, :] = np.tile(pst, 8)[None, :]
    koh = np.zeros((64, S), np.float32)
    for n in range(16):
        koh[n, n * 256:(n + 1) * 256] = 1.0
    von = np.zeros((128, 8, 128), np.float32)
    for h in range(8):
        if h % 2 == 0:
            von[:, h, 64:128] = 1.0
        else:
            von[:, h, 0:64] = 1.0
    return cbv, cfv, cs, gm, koh, von


def _params(inp):
    pr = np.zeros((128, NPAR), np.float32)

    def chunked(v):
        return np.ascontiguousarray(v.reshape(-1, 128).T)

    pr[:, P_N1:P_N1 + 8] = chunked(inp["ffn1_norm"][0])
    pr[:, P_MIX:P_MIX + 8] = chunked(inp["mix_norm"][0])
    pr[:, P_N2:P_N2 + 8] = chunked(inp["ffn2_norm"][0])
    conv = inp["gdn_conv"][0]
    pr[:, P_CONV:P_CONV + 48] = conv.T.reshape(12, 128, 4).transpose(1, 0, 2).reshape(128, 48)
    pr[:, P_ALOG:P_ALOG + 16] = np.tile(inp["gdn_a_log"][0], 4)[None, :]
    pr[:, P_DTB:P_DTB + 16] = np.tile(inp["gdn_dt_bias"][0], 4)[None, :]
    pr[:, P_ONORM:P_ONORM + 128] = inp["gdn_out_norm"][0][None, :]
    pr[:, P_QG] = np.tile(inp["moba_q_norm"][0], 2)
    pr[:, P_KG] = np.tile(inp["moba_k_norm"][0], 2)
    return pr


_CACHE = {}


def kernel(**inputs):
    inp = {k: np.asarray(v, dtype=np.float32) for k, v in inputs.items()}
    if "nc" not in _CACHE:
        _, _, order = build()
        _CACHE["nc"] = build(worder=order)[0]
        _CACHE["consts"] = _consts()
    nc = _CACHE["nc"]
    cbv, cfv, cs, gm, koh, von = _CACHE["consts"]
    pr = _params(inp)
    shared = {
        "w1g": np.ascontiguousarray(inp["ffn1_w_gate"][0]), "w1u": np.ascontiguousarray(inp["ffn1_w_up"][0]),
        "w1d": np.ascontiguousarray(inp["ffn1_w_down"][0]),
        "w2g": np.ascontiguousarray(inp["ffn2_w_gate"][0]), "w2u": np.ascontiguousarray(inp["ffn2_w_up"][0]),
        "w2d": np.ascontiguousarray(inp["ffn2_w_down"][0]),
        "win": np.ascontiguousarray(inp["w_in"][0]), "wout": np.ascontiguousarray(inp["w_out"][0]),
        "params": pr, "cf": cfv, "cb": cbv, "cossin": cs, "gmk": gm, "koh": koh, "vones": von,
    }
    x = inp["x"]
    in_maps = []
    for b in range(8):
        m = dict(shared)
        m["xT"] = np.ascontiguousarray(x[b].T)
        in_maps.append(m)
    res = run_bass_kernel_spmd(nc, in_maps, core_ids=list(range(8)))
    out = np.empty((8, S, D), np.float32)
    for b in range(8):
        out[b] = res.results[b]["yT"].T
    return out
```

```python
import contextlib
import math
import numpy as np
import concourse.bass as bass
import concourse.mybir as mybir
from concourse.bass_utils import run_bass_kernel_spmd

F32 = mybir.dt.float32
BF16 = mybir.dt.bfloat16
AF = mybir.ActivationFunctionType
ALU = mybir.AluOpType
AX = mybir.AxisListType

S = 4096
D = 1024
T = 512
NT = S // T
DFF = 2816
EPS = 1e-6
NEG = -30000.0
NPAR = 256

CB_ID, CB_ONES, CB_ONESBD, CB_MS, CB_MIT, CB_CM = 0, 128, 256, 384, 512, 640
NCB = 2688
CF_ID, CF_UI, CF_LST, CF_ONES, CF_RM = 0, 128, 256, 384, 512
NCF = 640
P_N1, P_MIX, P_N2, P_CONV, P_ALOG, P_DTB, P_ONORM, P_QG, P_KG = 0, 8, 16, 24, 72, 88, 104, 232, 233


class Eng:
    def __init__(self, name, eng, in_order=False):
        self.name = name
        self.eng = eng
        self.n = 0
        self.sem = None
        self.waited = {}
        self.in_order = in_order


class Buf:
    __slots__ = ("w", "r", "psum", "name")

    def __init__(self, name="", psum=False):
        self.w = None
        self.r = {}
        self.psum = psum
        self.name = name


def _deps(rd, wr, me):
    need = {}

    def add(e, c):
        if e is me and me.in_order:
            return
        if need.get(e, 0) < c:
            need[e] = c

    for b in rd:
        if b.w is not None:
            add(*b.w)
        if b.psum:
            for e, c in b.r.items():
                if e is not me:
                    add(e, c)
    for b in wr:
        if b.w is not None:
            add(*b.w)
        for e, c in b.r.items():
            add(e, c)
    return need


def op(me, fn, rd=(), wr=()):
    need = _deps(rd, wr, me)
    for e, c in need.items():
        if me.waited.get(e, 0) >= c:
            continue
        me.eng.wait_ge(e.sem, c)
        me.waited[e] = c
    ins = fn()
    me.n += 1
    ins.then_inc(me.sem, 1)
    for b in rd:
        b.r[me] = me.n
    for b in wr:
        b.w = (me, me.n)
        b.r = {}
    return ins


def dma(issuer, dsem, out_ap, in_ap, rd=(), wr=()):
    need = _deps(rd, wr, dsem)
    for e, c in need.items():
        if issuer.waited.get(e, 0) >= c:
            continue
        issuer.eng.wait_ge(e.sem, c)
        issuer.waited[e] = c
    ins = issuer.eng.dma_start(out=out_ap, in_=in_ap)
    dsem.n += 16
    ins.then_inc(dsem.sem, 16)
    for b in rd:
        b.r[dsem] = dsem.n
    for b in wr:
        b.w = (dsem, dsem.n)
        b.r = {}
    return ins


def build(ntiles=NT, debug=False, worder=None):
    nc = bass.Bass("TRN2", target_bir_lowering=False)
    dbg_outs = {}

    def din(name, shape, dt=F32):
        return nc.dram_tensor(name, list(shape), dt, kind="ExternalInput").ap()

    xT = din("xT", [D, S])
    yT = nc.dram_tensor("yT", [D, S], F32, kind="ExternalOutput").ap()
    w1g, w1u, w1d = din("w1g", [D, DFF]), din("w1u", [D, DFF]), din("w1d", [DFF, D])
    w2g, w2u, w2d = din("w2g", [D, DFF]), din("w2u", [D, DFF]), din("w2d", [DFF, D])
    win = din("win", [D, 3592])
    wout = din("wout", [D, D])
    params_d = din("params", [128, NPAR])
    cf_d = din("cf", [128, NCF])
    cb_d = din("cb", [128, NCB])
    cs_d = din("cossin", [128, 2, S])
    gm_d = din("gmk", [128, 16, 2, 256])
    koh_d = din("koh", [64, S])
    von_d = din("vones", [128, 8, 128])
    Kd = nc.dram_tensor("Kd", [8, 128, S], BF16).ap()
    Vd = nc.dram_tensor("Vd", [S, 8, 128], BF16).ap()

    es = contextlib.ExitStack()
    with es:
        def sb(name, shape, dt=F32):
            return es.enter_context(nc.sbuf_tensor("sb_" + name, list(shape), dt))

        PE = Eng("pe", nc.tensor, in_order=True)
        ACT = Eng("act", nc.scalar)
        DVE = Eng("dve", nc.vector)
        POOL = Eng("pool", nc.gpsimd)
        SP = Eng("sp", nc.sync)
        dsems = {}

        def dsem(name):
            if name not in dsems:
                e = Eng(name, None)
                e.sem = es.enter_context(nc.semaphore("d_" + name))
                dsems[name] = e
            return dsems[name]

        for e in (PE, ACT, DVE, POOL, SP):
            e.sem = es.enter_context(nc.semaphore("s_" + e.name))

        NB = 8
        NROT = 6
        banks = []
        for k in range(NB):
            t = es.enter_context(nc.psum_tensor(f"pb{k}", [128, 512], F32))
            banks.append((t, Buf(f"pb{k}", psum=True)))
        def mkpool(idxs):
            st = {"k": 0}

            def alloc():
                k = idxs[st["k"] % len(idxs)]
                st["k"] += 1
                return banks[k]
            return alloc

        P_ALL = mkpool(list(range(8)))
        P_FFN = mkpool([0, 1])
        P_GDN = mkpool([2, 3])
        bank = P_ALL

        class _BV:
            def __init__(self, t):
                self.t = t

            def __getitem__(self, key):
                assert key == (slice(None), slice(0, 128))
                return self.t[:, 0:64].bitcast(BF16)

        def bbank(alloc=None):
            t, B = (alloc or bank)()
            return _BV(t), B

        par = sb("par", [128, NPAR]); parB = Buf("par")
        cf = sb("cf", [128, NCF]); cb = sb("cb", [128, NCB], BF16); cB = Buf("consts")
        kmT = sb("kmT", [128, 8, 32], BF16); kmB = Buf("kmT")
        S_f = sb("S_f", [128, 4, 128]); S_b = sb("S_b", [128, 4, 128], BF16)
        SfB = [Buf(f"Sf{h}") for h in range(4)]; SbB = [Buf(f"Sb{h}") for h in range(4)]
        hist = sb("hist", [128, 12, 3]); histB = [Buf(f"hist{c}") for c in range(12)]
        class Ctx:
            pass
        ctxs = []
        for k in range(2):
            cx = Ctx()
            cx.xs = sb(f"xs{k}", [128, 8, T])
            cx.xsB = [Buf(f"xs{k}_{c}") for c in range(8)]
            ctxs.append(cx)
        hTf = sb("hTf", [128, 8, T], BF16); hBf = [Buf(f"hf{c}") for c in range(8)]
        hTm = sb("hTm", [128, 8, T], BF16); hBm = [Buf(f"hm{c}") for c in range(8)]
        bft2 = sb("bft2", [128, T], BF16); bft2B = Buf("bft2")
        actT = sb("actT", [128, 11, T], BF16); actB = [Buf(f"act{c}") for c in range(11)]
        NSLOT = 4
        wslots = [sb(f"wslot{k}", [128, 4096], BF16) for k in range(NSLOT)]
        wslotB = [Buf(f"wslot{k}") for k in range(NSLOT)]
        NTMP = 6
        tmps = [sb(f"tmp{k}", [128, T]) for k in range(NTMP)]
        tmpB = [Buf(f"tmp{k}") for k in range(NTMP)]

        def mktmp(idxs):
            st = {"k": 0}

            def alloc():
                k = idxs[st["k"] % len(idxs)]
                st["k"] += 1
                return tmps[k], tmpB[k]
            return alloc

        tmpF = mktmp([0, 1])
        tmp = mktmp([2, 3, 4, 5])
        tmpA = mktmp([2, 3])
        tmpB_ = mktmp([4, 5])

        pcx = sb("pcx", [128, T + 3]); pcxB = Buf("pcx")
        qT_b = sb("qT_b", [128, 4, T], BF16); qTB = [Buf(f"qT{h}") for h in range(4)]
        kT_b = sb("kT_b", [128, 4, T], BF16); kTB = [Buf(f"kT{h}") for h in range(4)]
        ktok = sb("ktok", [128, 4, 4, 128], BF16); ktokB = [[Buf() for _ in range(4)] for _ in range(4)]
        vtok = sb("vtok", [128, 4, 4, 128], BF16); vtokB = [[Buf() for _ in range(4)] for _ in range(4)]
        ztok = sb("ztok", [128, 4, T], BF16); ztokB = [Buf() for _ in range(4)]
        bft = sb("bft", [128, T], BF16); bftB = Buf("bft")
        qx = sb("qx", [128, 8, T], BF16); mqB = [Buf() for _ in range(4)]
        mkT = sb("mkT", [128, 4, T], BF16); mkB = [Buf() for _ in range(4)]
        vt = sb("vt", [128, 4, 512], BF16); vtB = Buf("vt")
        PTs = [sb(f"PT{k}", [128, T], BF16) for k in range(4)]; PTB = [Buf(f"PT{k}") for k in range(4)]
        rs, rsB = tmps[2], tmpB[2]
        cs = sb("cs", [128, 2, T]); csB = Buf("cs")
        gmt = sb("gmt", [128, 2, 2, 256]); gmtB = Buf("gmt")
        gmm = sb("gmm", [128, 256]); gmmB = Buf("gmm")
        bsel = sb("bsel", [128, 256]); bselB = Buf("bsel")
        top8 = sb("top8", [128, 8, 8]); top8B = Buf("top8")
        km12 = sb("km12", [128, 2]); km12B = Buf("km12")
        gat = sb("gat", [128, 4, 8]); gatB = Buf("gat")
        gg = sb("gg", [128, 4, 4]); ggB = Buf("gg")
        beta = sb("beta", [128, 4, 4]); betaB = Buf("beta")
        gtmp = sb("gtmp", [128, 4, 4]); gtmpB = Buf("gtmp")
        egs = sb("egs", [128, 12]); egsB = Buf("egs")
        bge = sb("bge", [128, 4]); bgeB = Buf("bge")
        ssq = sb("ssq", [128, 4]); ssqB = [Buf() for _ in range(4)]
        rsd = sb("rsd", [128, 4]); rsdB = [Buf() for _ in range(4)]
        mixT = sb("mixT", [128, 8, T], BF16); mixB = [Buf(f"mix{c}") for c in range(8)]
        Kp = [[sb(f"Kp{a}_{k}", [128, 512], BF16) for k in range(2)] for a in range(2)]
        KpB = [[Buf() for k in range(2)] for a in range(2)]
        Vp = [[sb(f"Vp{a}_{k}", [128, 4, 128], BF16) for k in range(2)] for a in range(2)]
        VpB = [[Buf() for k in range(2)] for a in range(2)]
        KdB = [Buf(f"Kd{h}") for h in range(8)]
        VdB = Buf("Vd")
        _gs = sb("Gs", [128, 128]); _gsB = Buf("Gs")
        Gs = [_gs] * 4; GsB = [_gsB] * 4
        Dst = [sb(f"Dst{h}", [128, 128]) for h in range(4)]; DstB = [Buf() for _ in range(4)]
        DTi = [sb(f"DTi{h}", [128, 128]) for h in range(4)]; DTiB = [Buf() for _ in range(4)]
        Ap = [sb(f"Ap{h}", [128, 128]) for h in range(4)]; ApB = [Buf() for _ in range(4)]
        Bp = [sb(f"Bp{h}", [128, 128]) for h in range(4)]; BpB = [Buf() for _ in range(4)]
        inT = [sb(f"inT{h}", [128, 128], BF16) for h in range(4)]; inTB = [Buf() for _ in range(4)]
        Xs = [sb(f"X{h}", [128, 256]) for h in range(4)]; XsB = [Buf() for _ in range(4)]
        kdec = [sb(f"kdec{h}", [128, 128], BF16) for h in range(4)]; kdecB = [Buf() for _ in range(4)]
        wT = [sb(f"wT{h}", [128, 128], BF16) for h in range(4)]; wTB = [Buf() for _ in range(4)]
        vnew = [sb(f"vnew{h}", [128, 128], BF16) for h in range(4)]; vnewB = [Buf() for _ in range(4)]
        ofp = [sb(f"ofp{h}", [128, 128]) for h in range(4)]; ofpB = [Buf() for _ in range(4)]
        otm = [sb(f"otm{h}", [128, 128]) for h in range(4)]; otmB = [Buf() for _ in range(4)]
        ogb = [sb(f"ogb{h}", [128, 128], BF16) for h in range(4)]; ogbB = [Buf() for _ in range(4)]

        ident_bf = cb[:, CB_ID:CB_ID + 128]
        ones_bf = cb[:, CB_ONES:CB_ONES + 128]
        ones_bd = cb[:, CB_ONESBD:CB_ONESBD + 128]
        maskS = cb[:, CB_MS:CB_MS + 128]
        maskIT = cb[:, CB_MIT:CB_MIT + 128]
        ident_f = cf[:, CF_ID:CF_ID + 128]
        Ui_f = cf[:, CF_UI:CF_UI + 128]
        Lst_f = cf[:, CF_LST:CF_LST + 128]
        ones_f = cf[:, CF_ONES:CF_ONES + 128]
        Rm_f = cf[:, CF_RM:CF_RM + 128]

        def cmask(jj):
            return cb[:, CB_CM + jj * 512:CB_CM + (jj + 1) * 512]

        def dump(name, ap, shape, bufs):
            if not debug:
                return
            d = nc.dram_tensor("dbg_" + name, list(shape), ap.dtype, kind="ExternalOutput").ap()
            dbg_outs[name] = d
            dma(SP, dsem("dbg"), d, ap, rd=bufs)

        dma(SP, dsem("c0"), par[:], params_d[:, :], wr=[parB])
        dma(SP, dsem("c1"), cf[:], cf_d[:, :], wr=[cB])
        for c0 in range(0, NCB, 1024):
            c1 = min(NCB, c0 + 1024)
            dma(POOL, dsem("c2"), cb[:, c0:c1], cb_d[:, c0:c1], wr=[cB])
        op(DVE, lambda: nc.vector.memset(hist[:], 0.0), wr=histB)
        op(DVE, lambda: nc.vector.memset(S_f[:], 0.0), wr=SfB)
        op(DVE, lambda: nc.vector.memset(S_b[:], 0.0), wr=SbB)
        op(DVE, lambda: nc.vector.memset(kmT[:], 0.0), wr=[kmB])
        op(DVE, lambda: nc.vector.memset(qx[:], 0.0), wr=mqB)
        op(ACT, lambda: nc.scalar.activation(out=par[:, P_ALOG:P_ALOG + 16], in_=par[:, P_ALOG:P_ALOG + 16], func=AF.Exp), rd=[parB], wr=[parB])
        op(DVE, lambda: nc.vector.tensor_scalar_mul(out=par[:, P_ALOG:P_ALOG + 16], in0=par[:, P_ALOG:P_ALOG + 16], scalar1=-1.0), rd=[parB], wr=[parB])

        WMAP = {"w1g": w1g, "w1u": w1u, "w1d": w1d, "w2g": w2g, "w2u": w2u, "w2d": w2d, "win": win, "wout": wout}

        def ffn_slabs(n):
            wg, wu, wd = f"w{n}g", f"w{n}u", f"w{n}d"
            for fh in range(2):
                f0 = fh * 1408
                for (n0, ncl) in ((0, 512), (512, 512), (1024, 384)):
                    yield (wg, 0, 8, f0 + n0, ncl)
                    yield (wu, 0, 8, f0 + n0, ncl)
                for dg in range(4):
                    yield (wd, f0, 11, dg * 256, 256)

        def inproj_slabs():
            for c0 in (0, 512, 1024, 1536):
                yield ("win", 0, 8, c0, 512)
            yield ("win", 0, 8, 2048, 8)
            for c0 in (2056, 2568, 3080):
                yield ("win", 0, 8, c0, 512)

        def outproj_slabs():
            for c0 in (0, 512):
                yield ("wout", 0, 8, c0, 512)

        record = worder is None
        wrec = []
        wq = []
        wst = {"n": 0, "next": 0}

        def w_load(k, spec):
            (wn, r0, kc, c0, ncl) = spec
            w = WMAP[wn]
            view = wslots[k][:, 0:kc * ncl].rearrange("p (c n) -> p c n", c=kc)
            src = w[r0:r0 + kc * 128, c0:c0 + ncl].rearrange("(c p) n -> p c n", p=128)
            dma(POOL, dsem(f"w{k}"), view, src, wr=[wslotB[k]])
            return (view, wslotB[k], k)

        def w_issue(k):
            if record or wst["next"] >= len(worder):
                return
            spec = worder[wst["next"]]
            wst["next"] += 1
            wq.append((spec, w_load(k, spec)))

        if not record:
            for k in range(NSLOT):
                w_issue(k)

        def init_kv():
            for h in range(8):
                for c0 in (0, 2048):
                    dma(POOL, dsem(f"kinit{h}"), Kd[h, 64:128, c0:c0 + 2048], koh_d[:, c0:c0 + 2048], wr=[KdB[h]])
            for t0 in range(0, S, 128):
                dma(POOL, dsem("vinit"), Vd[t0:t0 + 128, :, :], von_d[:, :, :], wr=[VdB])


        def wget(spec):
            if record:
                wrec.append(spec)
                k = wst["n"] % NSLOT
                wst["n"] += 1
                return w_load(k, spec)
            sp, v = wq.pop(0)
            assert sp == spec, (sp, spec)
            return v

        def wdone(k):
            w_issue(k)

        def mm(out, lhsT, rhs, start, stop, rd, wr, **kw):
            return op(PE, lambda: nc.tensor.matmul(out, lhsT=lhsT, rhs=rhs, start=start, stop=stop, **kw), rd=rd, wr=wr)

        def act(out, in_, func, rd, wr, **kw):
            return op(ACT, lambda: nc.scalar.activation(out=out, in_=in_, func=func, **kw), rd=rd, wr=wr)

        def sigmoid_inplace(buf_ap, in_ap, rd, B):
            act(buf_ap, in_ap, AF.Exp, rd=rd, wr=[B], scale=-1.0)
            act(buf_ap, buf_ap, AF.Ln, rd=[B], wr=[B], bias=1.0)
            act(buf_ap, buf_ap, AF.Exp, rd=[B], wr=[B], scale=-1.0)

        def rstd_from(out_ap, in_ap, scale, rd, B):
            act(out_ap, in_ap, AF.Ln, rd=rd, wr=[B], scale=scale, bias=EPS)
            act(out_ap, out_ap, AF.Exp, rd=[B], wr=[B], scale=-0.5)

        def barrier():
            engs = (PE, ACT, DVE, POOL)
            for a in engs:
                for b in engs:
                    if a is b or b.n == 0:
                        continue
                    if a.waited.get(b, 0) >= b.n:
                        continue
                    a.eng.wait_ge(b.sem, b.n)
                    a.waited[b] = b.n

        def norm_h(gcol, X, bank, mixer, talloc=None):
            xs, xsB = X.xs, X.xsB
            hT, hB = (hTm, hBm) if mixer else (hTf, hBf)
            talloc = talloc or (tmp if mixer else tmpF)
            pb, pB = bank()
            if mixer:
                for c in range(8):
                    act(bft2[:, :], xs[:, c, :], AF.Square, rd=[xsB[c]], wr=[bft2B])
                    mm(pb[:, :], ones_bf, bft2[:, :], c == 0, c == 7, rd=[bft2B, cB], wr=[pB])
                    if c % 2 == 1:
                        yield
            else:
                for c in range(8):
                    act(actT[:, c, :], xs[:, c, :], AF.Square, rd=[xsB[c]], wr=[actB[c]])
                yield
                for c in range(8):
                    mm(pb[:, :], ones_bf, actT[:, c, :], c == 0, c == 7, rd=[actB[c], cB], wr=[pB])
            r, rB = talloc()
            rstd_from(r[:, :], pb[:, :], 1.0 / D, [pB], rB)
            yield
            for c in range(8):
                op(DVE, lambda: nc.vector.scalar_tensor_tensor(out=hT[:, c, :], in0=xs[:, c, :], scalar=par[:, gcol + c:gcol + c + 1], in1=r[:, :], op0=ALU.mult, op1=ALU.mult),
                   rd=[xsB[c], rB, parB], wr=[hB[c]])
                if c % 4 == 3:
                    yield

        def ffn(n, gcol, X, bank=P_ALL):
            xs, xsB = X.xs, X.xsB
            hT, hB = hTf, hBf
            sl_ = ffn_slabs(n)
            yield from norm_h(gcol, X, bank, False)
            for fh in range(2):
                for (n0, ncl) in ((0, 512), (512, 512), (1024, 384)):
                    gv, gB, gk = wget(next(sl_))
                    uv, uB, uk = wget(next(sl_))
                    for j in range(ncl // 128):
                        fl = n0 // 128 + j
                        pg, pgB = bank()
                        pu, puB = bank()
                        for c in range(8):
                            mm(pg[:, :], gv[:, c, j * 128:(j + 1) * 128], hT[:, c, :], c == 0, c == 7, rd=[gB, hB[c]], wr=[pgB])
                        for c in range(8):
                            mm(pu[:, :], uv[:, c, j * 128:(j + 1) * 128], hT[:, c, :], c == 0, c == 7, rd=[uB, hB[c]], wr=[puB])
                        e, eB = tmpF()
                        sigmoid_inplace(e[:, :], pg[:, :], [pgB], eB)
                        t, tB = tmpF()
                        op(DVE, lambda: nc.vector.tensor_tensor(out=t[:, :], in0=pg[:, :], in1=e[:, :], op=ALU.mult), rd=[pgB, eB], wr=[tB])
                        op(DVE, lambda: nc.vector.tensor_tensor(out=actT[:, fl, :], in0=pu[:, :], in1=t[:, :], op=ALU.mult), rd=[puB, tB], wr=[actB[fl]])
                        yield
                    wdone(gk)
                    wdone(uk)
                for dg in range(4):
                    wv, wB, wk = wget(next(sl_))
                    for dd in range(2):
                        d = dg * 2 + dd
                        po, poB = bank()
                        for f in range(11):
                            mm(po[:, :], wv[:, f, dd * 128:(dd + 1) * 128], actT[:, f, :], f == 0, f == 10, rd=[wB, actB[f]], wr=[poB])
                        op(DVE, lambda: nc.vector.scalar_tensor_tensor(out=xs[:, d, :], in0=po[:, :], scalar=0.5, in1=xs[:, d, :], op0=ALU.mult, op1=ALU.add),
                           rd=[poB, xsB[d]], wr=[xsB[d]])
                        yield
                    wdone(wk)

        def gdn_qkv(bank, sl_):
            hT, hB = hTm, hBm
            for which in range(3):
                wv, wB, wk = wget(next(sl_))
                for hc in range(4):
                    c12 = which * 4 + hc
                    pb, pB = bank()
                    for c in range(8):
                        mm(pb[:, :], wv[:, c, hc * 128:(hc + 1) * 128], hT[:, c, :], c == 0, c == 7, rd=[wB, hB[c]], wr=[pB])
                    if hc == 3:
                        wdone(wk)
                    op(DVE, lambda: nc.vector.tensor_copy(pcx[:, 0:3], hist[:, c12, :]), rd=[histB[c12]], wr=[pcxB])
                    act(pcx[:, 3:T + 3], pb[:, :], AF.Copy, rd=[pB], wr=[pcxB])
                    op(DVE, lambda: nc.vector.tensor_copy(hist[:, c12, :], pcx[:, T:T + 3]), rd=[pcxB], wr=[histB[c12]])
                    ca, caB = tmpA()
                    cw = P_CONV + c12 * 4
                    op(DVE, lambda: nc.vector.tensor_scalar_mul(out=ca[:, :], in0=pcx[:, 0:T], scalar1=par[:, cw:cw + 1]), rd=[pcxB, parB], wr=[caB])
                    for k in range(1, 4):
                        op(DVE, lambda: nc.vector.scalar_tensor_tensor(out=ca[:, :], in0=pcx[:, k:k + T], scalar=par[:, cw + k:cw + k + 1], in1=ca[:, :], op0=ALU.mult, op1=ALU.add),
                           rd=[pcxB, parB, caB], wr=[caB])
                    e, eB = tmpA()
                    sigmoid_inplace(e[:, :], ca[:, :], [caB], eB)
                    if which == 2:
                        op(DVE, lambda: nc.vector.tensor_tensor(out=bft[:, :], in0=ca[:, :], in1=e[:, :], op=ALU.mult), rd=[caB, eB], wr=[bftB])
                        yield
                        for blk in range(4):
                            pt, ptB = bbank(bank)
                            op(PE, lambda: nc.tensor.transpose(out=pt[:, 0:128], in_=bft[:, blk * 128:(blk + 1) * 128], identity=ident_bf), rd=[bftB, cB], wr=[ptB])
                            act(vtok[:, blk, hc, :], pt[:, 0:128], AF.Copy, rd=[ptB], wr=[vtokB[blk][hc]])
                        yield
                        continue
                    op(DVE, lambda: nc.vector.tensor_tensor(out=ca[:, :], in0=ca[:, :], in1=e[:, :], op=ALU.mult), rd=[caB, eB], wr=[caB])
                    act(bft[:, :], ca[:, :], AF.Square, rd=[caB], wr=[bftB])
                    yield
                    p2, p2B = bank()
                    mm(p2[:, :], ones_bf, bft[:, :], True, True, rd=[bftB, cB], wr=[p2B])
                    rstd_from(e[:, :], p2[:, :], 1.0, [p2B], eB)
                    dst, dB = (qT_b, qTB) if which == 0 else (kT_b, kTB)
                    sc = (128.0 ** -0.5) if which == 0 else 1.0
                    op(DVE, lambda: nc.vector.scalar_tensor_tensor(out=dst[:, hc, :], in0=ca[:, :], scalar=sc, in1=e[:, :], op0=ALU.mult, op1=ALU.mult), rd=[caB, eB], wr=[dB[hc]])
                    yield
                    if which == 1:
                        for blk in range(4):
                            pt, ptB = bbank(bank)
                            op(PE, lambda: nc.tensor.transpose(out=pt[:, 0:128], in_=kT_b[:, hc, blk * 128:(blk + 1) * 128], identity=ident_bf), rd=[kTB[hc], cB], wr=[ptB])
                            act(ktok[:, blk, hc, :], pt[:, 0:128], AF.Copy, rd=[ptB], wr=[ktokB[blk][hc]])
                        yield

        def gdn_z_gates(bank, sl_):
            hT, hB = hTm, hBm
            wv, wB, wk = wget(next(sl_))
            for blk in range(4):
                pb, pB = bank()
                for c in range(8):
                    mm(pb[:, :], hT[:, c, blk * 128:(blk + 1) * 128], wv[:, c, :], c == 0, c == 7, rd=[wB, hB[c]], wr=[pB])
                e, eB = tmpB_()
                sigmoid_inplace(e[:, :], pb[:, :], [pB], eB)
                op(DVE, lambda: nc.vector.tensor_tensor(out=ztok[:, blk, :], in0=pb[:, :], in1=e[:, :], op=ALU.mult), rd=[pB, eB], wr=[ztokB[blk]])
                yield
            wdone(wk)
            wv, wB, wk = wget(next(sl_))
            pb, pB = bank()
            for blk in range(4):
                for c in range(8):
                    mm(pb[:, blk * 8:(blk + 1) * 8], hT[:, c, blk * 128:(blk + 1) * 128], wv[:, c, :], c == 0, c == 7, rd=[wB, hB[c]], wr=[pB])
            wdone(wk)
            act(gat[:, :, :], pb[:, 0:32].rearrange("p (b k) -> p b k", b=4), AF.Copy, rd=[pB], wr=[gatB])
            op(DVE, lambda: nc.vector.tensor_tensor(out=gtmp[:, :, :], in0=gat[:, :, 0:4], in1=par[:, P_DTB:P_DTB + 16].rearrange("p (b k) -> p b k", b=4), op=ALU.add), rd=[gatB, parB], wr=[gtmpB])
            act(gtmp[:, :, :], gtmp[:, :, :], AF.Exp, rd=[gtmpB], wr=[gtmpB])
            act(gtmp[:, :, :], gtmp[:, :, :], AF.Ln, rd=[gtmpB], wr=[gtmpB], bias=1.0)
            op(DVE, lambda: nc.vector.tensor_tensor(out=gg[:, :, :], in0=gtmp[:, :, :], in1=par[:, P_ALOG:P_ALOG + 16].rearrange("p (b k) -> p b k", b=4), op=ALU.mult), rd=[gtmpB, parB], wr=[ggB])
            sigmoid_inplace(beta[:, :, :], gat[:, :, 4:8], [gatB], betaB)
            yield

        def moba_qkv(i, bank, sl_):
            hT, hB = hTm, hBm
            dma(SP, dsem("cs"), cs[:], cs_d[:, :, i * T:(i + 1) * T], wr=[csB])
            dma(SP, dsem("gm"), gmt[:], gm_d[:, 2 * i:2 * i + 2, :, :], wr=[gmtB])
            for which in range(2):
                wv, wB, wk = wget(next(sl_))
                for c in range(4):
                    pb, pB = bank()
                    for cc in range(8):
                        mm(pb[:, :], wv[:, cc, c * 128:(c + 1) * 128], hT[:, cc, :], cc == 0, cc == 7, rd=[wB, hB[cc]], wr=[pB])
                    if c == 3:
                        wdone(wk)
                    xf, xfB = tmpB_()
                    act(xf[:, :], pb[:, :], AF.Copy, rd=[pB], wr=[xfB])
                    act(bft2[:, :], pb[:, :], AF.Square, rd=[pB], wr=[bft2B])
                    yield
                    p2, p2B = bank()
                    mm(p2[:, :], ones_bd, bft2[:, :], True, True, rd=[bft2B, cB], wr=[p2B])
                    r, rB = tmpB_()
                    rstd_from(r[:, :], p2[:, :], 1.0 / 64.0, [p2B], rB)
                    gc = P_QG if which == 0 else P_KG
                    op(DVE, lambda: nc.vector.scalar_tensor_tensor(out=xf[:, :], in0=xf[:, :], scalar=par[:, gc:gc + 1], in1=r[:, :], op0=ALU.mult, op1=ALU.mult), rd=[xfB, rB, parB], wr=[xfB])
                    yield
                    p3, p3B = bank()
                    mm(p3[:, :], Rm_f, xf[:, :], True, True, rd=[xfB, cB], wr=[p3B])
                    op(DVE, lambda: nc.vector.tensor_tensor(out=r[:, :], in0=p3[:, :], in1=cs[:, 1, :], op=ALU.mult), rd=[p3B, csB], wr=[rB])
                    op(DVE, lambda: nc.vector.tensor_tensor(out=xf[:, :], in0=xf[:, :], in1=cs[:, 0, :], op=ALU.mult), rd=[xfB, csB], wr=[xfB])
                    op(DVE, lambda: nc.vector.tensor_tensor(out=xf[:, :], in0=xf[:, :], in1=r[:, :], op=ALU.add), rd=[xfB, rB], wr=[xfB])
                    if which == 0:
                        act(qx[0:64, 2 * c, :], xf[0:64, :], AF.Copy, rd=[xfB], wr=[mqB[c]])
                        act(qx[0:64, 2 * c + 1, :], xf[64:128, :], AF.Copy, rd=[xfB, mqB[c]], wr=[mqB[c]])
                    else:
                        act(mkT[:, c, :], xf[:, :], AF.Copy, rd=[xfB], wr=[mkB[c]])
                        for hp in range(2):
                            dma(SP, dsem(f"kst{2 * c + hp}"), Kd[2 * c + hp, 0:64, i * T:(i + 1) * T], mkT[hp * 64:(hp + 1) * 64, c, :], rd=[mkB[c]], wr=[KdB[2 * c + hp]])
                        op(DVE, lambda: nc.vector.tensor_reduce(out=km12[:, 0:1], in_=xf[:, 0:256], axis=AX.X, op=ALU.add), rd=[xfB], wr=[km12B])
                        op(DVE, lambda: nc.vector.tensor_reduce(out=km12[:, 1:2], in_=xf[:, 256:512], axis=AX.X, op=ALU.add), rd=[xfB, km12B], wr=[km12B])
                        act(kmT[0:64, 2 * c, 2 * i:2 * i + 2], km12[0:64, 0:2], AF.Copy, rd=[km12B], wr=[kmB], scale=1.0 / 256.0)
                        act(kmT[0:64, 2 * c + 1, 2 * i:2 * i + 2], km12[64:128, 0:2], AF.Copy, rd=[km12B, kmB], wr=[kmB], scale=1.0 / 256.0)
                    yield
            wv, wB, wk = wget(next(sl_))
            for sub in range(4):
                pb, pB = bank()
                for cc in range(8):
                    mm(pb[:, :], hT[:, cc, sub * 128:(sub + 1) * 128], wv[:, cc, :], cc == 0, cc == 7, rd=[wB, hB[cc]], wr=[pB])
                act(vt[:, sub, :], pb[:, :], AF.Copy, rd=[pB], wr=[vtB])
                yield
            wdone(wk)
            for h in range(8):
                off = 0 if h % 2 == 0 else 64
                dma(SP, dsem("vst"), Vd[i * T:(i + 1) * T, h, off:off + 64].rearrange("(s p) f -> p s f", p=128), vt[:, :, h * 64:(h + 1) * 64], rd=[vtB], wr=[VdB])

        def moba_gate(i, bank):
            for sub in range(4):
                ow = sub // 2
                pb, pB = bank()
                for h in range(8):
                    c, hp = h // 2, h % 2
                    mm(pb[:, h * 32:(h + 1) * 32], qx[:, h, sub * 128:(sub + 1) * 128], kmT[:, h, :], True, True,
                       rd=[mqB[c], kmB], wr=[pB])
                op(DVE, lambda: nc.vector.tensor_tensor(out=gmm[:, :], in0=pb[:, 0:256], in1=gmt[:, ow, 0, :], op=ALU.add), rd=[pB, gmtB], wr=[gmmB])
                for h in range(8):
                    op(DVE, lambda: nc.vector.max(out=top8[:, h, :], in_=gmm[:, h * 32:(h + 1) * 32]), rd=[gmmB], wr=[top8B])
                for h in range(8):
                    op(DVE, lambda: nc.vector.tensor_scalar(out=bsel[:, h * 32:(h + 1) * 32], in0=gmm[:, h * 32:(h + 1) * 32], scalar1=top8[:, h, 2:3], scalar2=1.0, op0=ALU.is_ge, op1=ALU.subtract),
                       rd=[gmmB, top8B], wr=[bselB])
                op(DVE, lambda: nc.vector.scalar_tensor_tensor(out=bsel[:, :], in0=bsel[:, :], scalar=-NEG, in1=gmt[:, ow, 1, :], op0=ALU.mult, op1=ALU.mult), rd=[bselB, gmtB], wr=[bselB])
                yield
                for g2 in range(2):
                    pt, ptB = bank()
                    op(PE, lambda: nc.tensor.transpose(out=pt[:, 0:128], in_=bsel[:, g2 * 128:(g2 + 1) * 128], identity=ident_f), rd=[bselB, cB], wr=[ptB])
                    for g in range(4):
                        hh = g2 * 4 + g
                        act(qx[64:80, hh, sub * 128:(sub + 1) * 128], pt[g * 32:g * 32 + 16, 0:128], AF.Copy, rd=[ptB, mqB[hh // 2]], wr=[mqB[hh // 2]])
                yield

        def moba_stream(i, sid, heads):
            nj = 4 * i + 4
            npc = (nj + 3) // 4
            ps_, psB = banks[4 + sid]
            po, poB = banks[6 + sid]
            ring = {"k": 0}
            pieces = [(h, pc) for h in heads for pc in range(npc)]

            def load(idx):
                h, pc = pieces[idx]
                k = idx % 2
                nk = min(4, nj - pc * 4)
                dma(SP, dsem(f"kl{sid}{k}"), Kp[sid][k][:, 0:nk * 128], Kd[h, :, pc * 512:pc * 512 + nk * 128], rd=[KdB[h]], wr=[KpB[sid][k]])
                dma(SP, dsem(f"vl{sid}{k}"), Vp[sid][k][:, 0:nk, :], Vd[pc * 512:pc * 512 + nk * 128, h, :].rearrange("(j p) f -> p j f", p=128), rd=[VdB], wr=[VpB[sid][k]])

            load(0)
            pend = None
            ptk = 0
            for idx, (h, pc) in enumerate(pieces):
                if pend is not None:
                    pend()
                    pend = None
                if idx + 1 < len(pieces):
                    load(idx + 1)
                k = idx % 2
                c, hp = h // 2, h % 2
                nk = min(4, nj - pc * 4)
                for jj in range(nk):
                    j = pc * 4 + jj
                    diag = j >= 4 * i
                    mm(ps_[:, :], Kp[sid][k][:, jj * 128:(jj + 1) * 128], qx[:, h, :], True, not diag, rd=[KpB[sid][k], mqB[c]], wr=[psB])
                    if diag:
                        mm(ps_[:, :], ident_bf, cmask(j - 4 * i), False, True, rd=[cB], wr=[psB])
                    if pend is not None:
                        pend()
                        pend = None
                    PT, PB = PTs[sid * 2 + ptk % 2], PTB[sid * 2 + ptk % 2]
                    ptk += 1
                    act(PT[:, :], ps_[:, :], AF.Exp, rd=[psB], wr=[PB], scale=0.125)

                    def pv(PT=PT, PB=PB, j=j, jj=jj, k=k):
                        mm(po[:, :], Vp[sid][k][:, jj, :], PT[:, :], j == 0, j == nj - 1, rd=[VpB[sid][k], PB], wr=[poB])
                    pend = pv
                    if j == nj - 1:
                        pend()
                        pend = None
                        osl = slice(hp * 64, (hp + 1) * 64)
                        ssl = slice((1 - hp) * 64, (2 - hp) * 64)
                        act(rs[osl, :], po[ssl, :], AF.Ln, rd=[poB], wr=[rsB])
                        act(rs[osl, :], rs[osl, :], AF.Exp, rd=[rsB], wr=[rsB], scale=-1.0)
                        op(DVE, lambda: nc.vector.tensor_tensor(out=mixT[osl, 4 + c, :], in0=po[osl, :], in1=rs[osl, :], op=ALU.mult), rd=[poB, rsB, mixB[4 + c]], wr=[mixB[4 + c]])
                    yield

        def gdn_block(blk, gb=P_GDN):
            bs = slice(blk * 128, (blk + 1) * 128)
            pb, pB = gb()
            mm(pb[:, 0:4], Ui_f, gg[:, blk, :], True, True, rd=[ggB, cB], wr=[pB])
            mm(pb[:, 4:8], Lst_f, gg[:, blk, :], True, True, rd=[ggB, cB], wr=[pB])
            mm(pb[:, 8:12], ones_f, gg[:, blk, :], True, True, rd=[ggB, cB], wr=[pB])
            act(egs[:, :], pb[:, 0:12], AF.Exp, rd=[pB], wr=[egsB])
            op(DVE, lambda: nc.vector.tensor_tensor(out=bge[:, :], in0=beta[:, blk, :], in1=egs[:, 0:4], op=ALU.mult), rd=[betaB, egsB], wr=[bgeB])
            yield

            def sD(h):
                op(DVE, lambda: nc.vector.tensor_scalar_mul(out=Gs[h][:, :], in0=Lst_f, scalar1=gg[:, blk, h:h + 1]), rd=[cB, ggB], wr=[GsB[h]])
                p1, p1B = gb()
                mm(p1[:, 0:128], Ui_f, Gs[h][:, :], True, False, rd=[cB, GsB[h]], wr=[p1B])
                mm(p1[:, 0:128], ident_bf, maskS, False, True, rd=[cB], wr=[p1B])
                mm(p1[:, 128:256], Gs[h][:, :], Ui_f, True, False, rd=[cB, GsB[h]], wr=[p1B])
                mm(p1[:, 128:256], ident_bf, maskIT, False, True, rd=[cB], wr=[p1B])
                act(Dst[h][:, :], p1[:, 0:128], AF.Exp, rd=[p1B], wr=[DstB[h]])
                act(DTi[h][:, :], p1[:, 128:256], AF.Exp, rd=[p1B], wr=[DTiB[h]])

            def sG(h):
                p2, p2B = gb()
                mm(p2[:, 0:128], kT_b[:, h, bs], kT_b[:, h, bs], True, True, rd=[kTB[h]], wr=[p2B])
                mm(p2[:, 128:256], kT_b[:, h, bs], qT_b[:, h, bs], True, True, rd=[kTB[h], qTB[h]], wr=[p2B])
                op(DVE, lambda: nc.vector.scalar_tensor_tensor(out=Ap[h][:, :], in0=p2[:, 0:128], scalar=beta[:, blk, h:h + 1], in1=Dst[h][:, :], op0=ALU.mult, op1=ALU.mult),
                   rd=[p2B, betaB, DstB[h]], wr=[ApB[h]])
                op(DVE, lambda: nc.vector.tensor_tensor(out=inT[h][:, :], in0=p2[:, 128:256], in1=DTi[h][:, :], op=ALU.mult), rd=[p2B, DTiB[h]], wr=[inTB[h]])
                op(DVE, lambda: nc.vector.tensor_scalar_mul(out=Xs[h][:, 0:128], in0=vtok[:, blk, h, :], scalar1=beta[:, blk, h:h + 1]), rd=[vtokB[blk][h], betaB], wr=[XsB[h]])
                op(DVE, lambda: nc.vector.tensor_scalar_mul(out=Xs[h][:, 128:256], in0=ktok[:, blk, h, :], scalar1=bge[:, h:h + 1]), rd=[ktokB[blk][h], bgeB, XsB[h]], wr=[XsB[h]])
                op(DVE, lambda: nc.vector.tensor_scalar_mul(out=kdec[h][:, :], in0=ktok[:, blk, h, :], scalar1=egs[:, 4 + h:5 + h]), rd=[ktokB[blk][h], egsB], wr=[kdecB[h]])

            def sT(h):
                pt, ptB = gb()
                op(PE, lambda: nc.tensor.transpose(out=pt[:, 0:128], in_=Ap[h][:, :], identity=ident_f), rd=[ApB[h], cB], wr=[ptB])
                act(Bp[h][:, :], pt[:, 0:128], AF.Copy, rd=[ptB], wr=[BpB[h]])

            def mk_solve(s):
                def f(h):
                    pk, pkB = gb()
                    mm(pk[:, 0:256], Bp[h][:, :], Xs[h][:, :], True, True, rd=[BpB[h], XsB[h]], wr=[pkB])
                    if s < 5:
                        mm(pk[:, 256:384], Bp[h][:, :], Ap[h][:, :], True, True, rd=[BpB[h], ApB[h]], wr=[pkB])
                    if s < 6:
                        mm(pk[:, 384:512], Ap[h][:, :], Bp[h][:, :], True, True, rd=[BpB[h], ApB[h]], wr=[pkB])
                    if s == 0:
                        op(DVE, lambda: nc.vector.scalar_tensor_tensor(out=Xs[h][:, :], in0=pk[:, 0:256], scalar=-1.0, in1=Xs[h][:, :], op0=ALU.mult, op1=ALU.add),
                           rd=[pkB, XsB[h]], wr=[XsB[h]])
                    else:
                        op(DVE, lambda: nc.vector.tensor_tensor(out=Xs[h][:, :], in0=pk[:, 0:256], in1=Xs[h][:, :], op=ALU.add),
                           rd=[pkB, XsB[h]], wr=[XsB[h]])
                    if s < 6:
                        act(Bp[h][:, :], pk[:, 384:512], AF.Copy, rd=[pkB], wr=[BpB[h]])
                    if s < 5:
                        act(Ap[h][:, :], pk[:, 256:384], AF.Copy, rd=[pkB], wr=[ApB[h]])
                return f

            def sW(h):
                pt, ptB = gb()
                op(PE, lambda: nc.tensor.transpose(out=pt[:, 0:128], in_=Xs[h][:, 128:256], identity=ident_f), rd=[XsB[h], cB], wr=[ptB])
                act(wT[h][:, :], pt[:, 0:128], AF.Copy, rd=[ptB], wr=[wTB[h]])

            def sV(h):
                pv, pvB = gb()
                mm(pv[:, 0:128], wT[h][:, :], S_b[:, h, :], True, True, rd=[wTB[h], SbB[h]], wr=[pvB])
                op(DVE, lambda: nc.vector.scalar_tensor_tensor(out=vnew[h][:, :], in0=pv[:, 0:128], scalar=-1.0, in1=Xs[h][:, 0:128], op0=ALU.mult, op1=ALU.add), rd=[pvB, XsB[h]], wr=[vnewB[h]])

            def sO(h):
                po, poB = gb()
                mm(po[:, 0:128], qT_b[:, h, bs], S_b[:, h, :], True, True, rd=[qTB[h], SbB[h]], wr=[poB])
                mm(po[:, 128:256], inT[h][:, :], vnew[h][:, :], True, True, rd=[inTB[h], vnewB[h]], wr=[poB])
                mm(po[:, 256:384], kdec[h][:, :], vnew[h][:, :], True, True, rd=[kdecB[h], vnewB[h]], wr=[poB])
                act(otm[h][:, :], po[:, 0:128], AF.Copy, rd=[poB, egsB], wr=[otmB[h]], scale=egs[:, h:h + 1])
                op(DVE, lambda: nc.vector.scalar_tensor_tensor(out=S_f[:, h, :], in0=S_f[:, h, :], scalar=egs[:, 8 + h:9 + h], in1=po[:, 256:384], op0=ALU.mult, op1=ALU.add),
                   rd=[SfB[h], egsB, poB], wr=[SfB[h]])
                op(DVE, lambda: nc.vector.tensor_tensor(out=ofp[h][:, :], in0=po[:, 128:256], in1=otm[h][:, :], op=ALU.add), rd=[poB, otmB[h]], wr=[ofpB[h]])
                act(S_b[:, h, :], S_f[:, h, :], AF.Copy, rd=[SfB[h]], wr=[SbB[h]])
                op(DVE, lambda: nc.vector.memset(ssq[:, h:h + 1], 0.0), wr=[ssqB[h]])
                act(otm[h][:, :], ofp[h][:, :], AF.Square, rd=[ofpB[h], ssqB[h]], wr=[otmB[h], ssqB[h]], accum_out=ssq[:, h:h + 1])
                rstd_from(rsd[:, h:h + 1], ssq[:, h:h + 1], 1.0 / 128.0, [ssqB[h]], rsdB[h])
                op(DVE, lambda: nc.vector.scalar_tensor_tensor(out=ofp[h][:, :], in0=ofp[h][:, :], scalar=rsd[:, h:h + 1], in1=par[:, P_ONORM:P_ONORM + 128], op0=ALU.mult, op1=ALU.mult),
                   rd=[ofpB[h], rsdB[h], parB], wr=[ofpB[h]])
                op(DVE, lambda: nc.vector.tensor_tensor(out=ogb[h][:, :], in0=ofp[h][:, :], in1=ztok[:, blk, h * 128:(h + 1) * 128], op=ALU.mult), rd=[ofpB[h], ztokB[blk]], wr=[ogbB[h]])

            def sX(h):
                pt, ptB = bbank(gb)
                op(PE, lambda: nc.tensor.transpose(out=pt[:, 0:128], in_=ogb[h][:, :], identity=ident_bf), rd=[ogbB[h], cB], wr=[ptB])
                act(mixT[:, h, bs], pt[:, 0:128], AF.Copy, rd=[ptB], wr=[mixB[h]])

            steps = [sD, sG, sT] + [mk_solve(s) for s in range(7)] + [sW, sV, sO, sX]
            for st in steps:
                for pair in ((0, 1), (2, 3)):
                    for h in pair:
                        st(h)
                    yield

        def out_proj(X, bank=P_ALL):
            xs, xsB = X.xs, X.xsB
            sl_ = outproj_slabs()
            for dgp in range(2):
                wv, wB, wk = wget(next(sl_))
                for dd in range(4):
                    d = dgp * 4 + dd
                    pb, pB = bank()
                    for m in range(8):
                        mm(pb[:, :], wv[:, m, dd * 128:(dd + 1) * 128], mixT[:, m, :], m == 0, m == 7, rd=[wB, mixB[m]], wr=[pB])
                    op(DVE, lambda: nc.vector.tensor_tensor(out=xs[:, d, :], in0=pb[:, :], in1=xs[:, d, :], op=ALU.add), rd=[pB, xsB[d]], wr=[xsB[d]])
                    yield
                wdone(wk)

        def run(*gens):
            gens = [g for g in gens if g is not None]
            dead = set()
            while len(dead) < len(set(map(id, gens))):
                for g in gens:
                    if id(g) in dead:
                        continue
                    try:
                        next(g)
                    except StopIteration:
                        dead.add(id(g))

        def seq(*fns):
            for f in fns:
                r = f()
                if r is not None:
                    yield from r

        def load_x(i, X):
            dma(SP, dsem(f"xl{i % 2}"), X.xs[:, :, :], xT[:, i * T:(i + 1) * T].rearrange("(c p) t -> p c t", p=128), wr=X.xsB)

        def gdn_all():
            for blk in range(4):
                yield from gdn_block(blk)

        P_F3 = mkpool([0, 1, 2])
        P_I5 = mkpool([3, 4, 5, 6, 7])

        def rr(*gens):
            gens = list(gens)
            while gens:
                for g in list(gens):
                    try:
                        next(g)
                        yield
                    except StopIteration:
                        gens.remove(g)

        def inproj(i, X, bank, do_norm=True):
            sl_ = list(inproj_slabs())
            s1 = iter(sl_[0:3])
            s2 = iter(sl_[3:8])
            if do_norm:
                yield from norm_h(P_MIX, X, bank, True)
            yield from rr(gdn_qkv(bank, s1), seq2(lambda: gdn_z_gates(bank, s2), lambda: moba_qkv(i, bank, s2), lambda: moba_gate(i, bank)))

        def seq2(*fns):
            for f in fns:
                yield from f()

        load_x(0, ctxs[0])
        run(ffn(1, P_N1, ctxs[0]))
        init_kv()
        if debug:
            dump("x1", ctxs[0].xs[:, :, :], [128, 8, T], ctxs[0].xsB)
        run(inproj(0, ctxs[0], P_ALL))
        for i in range(ntiles):
            X = ctxs[i % 2]
            Xn = ctxs[(i + 1) % 2]
            last = i + 1 >= ntiles
            if not last:
                load_x(i + 1, Xn)
            g_gdn = gdn_all()
            run(g_gdn, moba_stream(i, 0, [0, 2, 4, 6]), moba_stream(i, 1, [1, 3, 5, 7]), g_gdn, None if last else seq2(lambda: ffn(1, P_N1, Xn, P_FFN), lambda: norm_h(P_MIX, Xn, P_FFN, True, tmpF)))
            if debug and i == 0:
                dump("mixT", mixT[:, :, :], [128, 8, T], mixB)
            if last:
                run(out_proj(X))
                run(ffn(2, P_N2, X, P_ALL))
            else:
                g_in = inproj(i + 1, Xn, P_I5, do_norm=False)
                run(seq2(lambda: out_proj(X, P_F3), lambda: ffn(2, P_N2, X, P_F3)), g_in, g_in)
            dma(SP, dsem("st"), yT[:, i * T:(i + 1) * T].rearrange("(c p) t -> p c t", p=128), X.xs[:, :, :], rd=X.xsB)
        for nm, e in dsems.items():
            if nm.startswith("w") and e.n > 0:
                POOL.eng.wait_ge(e.sem, e.n)
        SP.eng.wait_ge(dsems["st"].sem, dsems["st"].n)
        if debug and "dbg" in dsems:
            SP.eng.wait_ge(dsems["dbg"].sem, dsems["dbg"].n)
    return nc, dbg_outs, wrec


def _consts():
    cbv = np.zeros((128, NCB), np.float32)
    cfv = np.zeros((128, NCF), np.float32)
    I = np.eye(128, dtype=np.float32)
    p = np.arange(128)
    cbv[:, CB_ID:CB_ID + 128] = I
    cbv[:, CB_ONES:CB_ONES + 128] = 1.0
    cbv[:, CB_ONESBD:CB_ONESBD + 128] = (p[:, None] // 64 == p[None, :] // 64)
    cbv[:, CB_MS:CB_MS + 128] = np.where(p[:, None] > p[None, :], 0.0, NEG)
    cbv[:, CB_MIT:CB_MIT + 128] = np.where(p[None, :] >= p[:, None], 0.0, NEG)
    q = np.arange(512)
    cm = np.zeros((128, 4, 512), np.float32)
    for jj in range(4):
        cm[:, jj, :] = np.where((jj * 128 + p)[:, None] <= q[None, :], 0.0, NEG)
    cbv[:, CB_CM:CB_CM + 2048] = cm.reshape(128, 2048)
    cfv[:, CF_ID:CF_ID + 128] = I
    cfv[:, CF_UI:CF_UI + 128] = (p[:, None] <= p[None, :])
    cfv[:, CF_LST:CF_LST + 128] = (p[:, None] > p[None, :])
    cfv[:, CF_ONES:CF_ONES + 128] = 1.0
    rm = np.zeros((128, 128), np.float32)
    for pp in range(128):
        d = pp % 64
        if d < 8:
            rm[pp + 8, pp] = -1.0
        elif d < 16:
            rm[pp - 8, pp] = 1.0
    cfv[:, CF_RM:CF_RM + 128] = rm
    half = 8
    inv_freq = np.power(np.float32(500000.0), -np.arange(half, dtype=np.float32) * 2.0 / 16.0).astype(np.float32)
    pos = np.arange(S, dtype=np.float32)
    ang = pos[:, None] * inv_freq[None, :]
    cosv, sinv = np.cos(ang).astype(np.float32), np.sin(ang).astype(np.float32)
    cs = np.zeros((128, 2, S), np.float32)
    cs[:, 0, :] = 1.0
    for pp in range(128):
        d = pp % 64
        if d < 16:
            cs[pp, 0, :] = cosv[:, d % 8]
            cs[pp, 1, :] = sinv[:, d % 8]
    gm = np.zeros((128, 16, 2, 256), np.float32)
    n = np.arange(32)
    for own in range(16):
        row = np.where(n < min(own, 16), 0.0, NEG).astype(np.float32)
        row[16:] = NEG
        pst = (n < own).astype(np.float32)
        pst[16:] = 0.0
        gm[:, own, 0, :] = np.tile(row, 8)[None, :]
        gm[:, own, 1, :] = np.tile(pst, 8)[None, :]
    koh = np.zeros((64, S), np.float32)
    for n in range(16):
        koh[n, n * 256:(n + 1) * 256] = 1.0
    von = np.zeros((128, 8, 128), np.float32)
    for h in range(8):
        if h % 2 == 0:
            von[:, h, 64:128] = 1.0
        else:
            von[:, h, 0:64] = 1.0
    return cbv, cfv, cs, gm, koh, von


def _params(inp):
    pr = np.zeros((128, NPAR), np.float32)

    def chunked(v):
        return np.ascontiguousarray(v.reshape(-1, 128).T)

    pr[:, P_N1:P_N1 + 8] = chunked(inp["ffn1_norm"][0])
    pr[:, P_MIX:P_MIX + 8] = chunked(inp["mix_norm"][0])
    pr[:, P_N2:P_N2 + 8] = chunked(inp["ffn2_norm"][0])
    conv = inp["gdn_conv"][0]
    pr[:, P_CONV:P_CONV + 48] = conv.T.reshape(12, 128, 4).transpose(1, 0, 2).reshape(128, 48)
    pr[:, P_ALOG:P_ALOG + 16] = np.tile(inp["gdn_a_log"][0], 4)[None, :]
    pr[:, P_DTB:P_DTB + 16] = np.tile(inp["gdn_dt_bias"][0], 4)[None, :]
    pr[:, P_ONORM:P_ONORM + 128] = inp["gdn_out_norm"][0][None, :]
    pr[:, P_QG] = np.tile(inp["moba_q_norm"][0], 2)
    pr[:, P_KG] = np.tile(inp["moba_k_norm"][0], 2)
    return pr


_CACHE = {}


def kernel(**inputs):
    inp = {k: np.asarray(v, dtype=np.float32) for k, v in inputs.items()}
    if "nc" not in _CACHE:
        _, _, order = build()
        _CACHE["nc"] = build(worder=order)[0]
        _CACHE["consts"] = _consts()
    nc = _CACHE["nc"]
    cbv, cfv, cs, gm, koh, von = _CACHE["consts"]
    pr = _params(inp)
    shared = {
        "w1g": np.ascontiguousarray(inp["ffn1_w_gate"][0]), "w1u": np.ascontiguousarray(inp["ffn1_w_up"][0]),
        "w1d": np.ascontiguousarray(inp["ffn1_w_down"][0]),
        "w2g": np.ascontiguousarray(inp["ffn2_w_gate"][0]), "w2u": np.ascontiguousarray(inp["ffn2_w_up"][0]),
        "w2d": np.ascontiguousarray(inp["ffn2_w_down"][0]),
        "win": np.ascontiguousarray(inp["w_in"][0]), "wout": np.ascontiguousarray(inp["w_out"][0]),
        "params": pr, "cf": cfv, "cb": cbv, "cossin": cs, "gmk": gm, "koh": koh, "vones": von,
    }
    x = inp["x"]
    in_maps = []
    for b in range(8):
        m = dict(shared)
        m["xT"] = np.ascontiguousarray(x[b].T)
        in_maps.append(m)
    res = run_bass_kernel_spmd(nc, in_maps, core_ids=list(range(8)))
    out = np.empty((8, S, D), np.float32)
    for b in range(8):
        out[b] = res.results[b]["yT"].T
    return out
```
